# Optimizing a Trainium2 kernel written in Bass

```python
import jax, jax.numpy as jnp
from jax import lax
import numpy as np

D_MODEL = 2048
BATCH = 4
SEQ = 4096
DEPTH = 2

ROPE_THETA = 500000.0
NORM_EPS = 1e-6
Q_BLOCK = 128
D_FF = 4 * D_MODEL
N_BRANCH = 3

MLA_HEADS = 16
MLA_Q_LORA = 512
MLA_KV_LORA = 512
MLA_NOPE = 128
MLA_ROPE = 64
MLA_V = 128

DSA_HEADS = 16
DSA_KV_HEADS = 4
DSA_HEAD_DIM = 128
DSA_ROT = DSA_HEAD_DIM // 4
IDX_HEADS = 16
IDX_HEAD_DIM = 64
IDX_ROT = IDX_HEAD_DIM // 4
INDEX_TOPK = 256

CONV_CH = D_MODEL
CONV_WIDTH = 31

IN_WIDTHS = [
    MLA_Q_LORA,
    MLA_KV_LORA,
    MLA_ROPE,
    DSA_HEADS * DSA_HEAD_DIM,
    DSA_KV_HEADS * DSA_HEAD_DIM,
    DSA_KV_HEADS * DSA_HEAD_DIM,
    IDX_HEADS * IDX_HEAD_DIM,
    IDX_HEAD_DIM,
    IDX_HEADS,
    2 * CONV_CH,
    N_BRANCH * D_MODEL,
]
IN_COLS = int(sum(IN_WIDTHS))
IN_SPLITS = [int(v) for v in np.cumsum(IN_WIDTHS)[:-1]]

kernel_name = "hybrid_mla_dsa_conformer_gated"


def rms_norm(x, g):
    xf = x.astype(jnp.float32)
    y = xf * lax.rsqrt(jnp.mean(xf * xf, axis=-1, keepdims=True) + NORM_EPS)
    return (y * g.astype(jnp.float32)).astype(x.dtype)


def layer_norm(x, g, b):
    xf = x.astype(jnp.float32)
    mu = jnp.mean(xf, axis=-1, keepdims=True)
    xc = xf - mu
    var = jnp.mean(xc * xc, axis=-1, keepdims=True)
    y = xc * lax.rsqrt(var + NORM_EPS) * g.astype(jnp.float32) + b.astype(jnp.float32)
    return y.astype(x.dtype)


def rope_tables(seq_len, rot_dim):
    inv = ROPE_THETA ** (-jnp.arange(0, rot_dim, 2, dtype=jnp.float32) / rot_dim)
    ang = jnp.arange(seq_len, dtype=jnp.float32)[:, None] * inv[None, :]
    return jnp.cos(ang), jnp.sin(ang)


def apply_rotary(x, cos, sin, rot_dim):
    half = rot_dim // 2
    xr = x[..., :rot_dim].astype(jnp.float32)
    x1, x2 = xr[..., :half], xr[..., half:]
    shape = (1, cos.shape[0]) + (1,) * (x.ndim - 3) + (half,)
    c, s = cos.reshape(shape), sin.reshape(shape)
    rot = jnp.concatenate([x1 * c - x2 * s, x2 * c + x1 * s], axis=-1).astype(x.dtype)
    return jnp.concatenate([rot, x[..., rot_dim:]], axis=-1)


def to_blocks(a):
    b, t = a.shape[:2]
    a = a.reshape((b, t // Q_BLOCK, Q_BLOCK) + a.shape[2:])
    return jnp.moveaxis(a, 1, 0)


def from_blocks(a):
    a = jnp.moveaxis(a, 0, 1)
    return a.reshape((a.shape[0], a.shape[1] * a.shape[2]) + a.shape[3:])


def dense_causal_attention(q, k, v, scale):
    t = q.shape[1]
    key_pos = jnp.arange(k.shape[1])

    def one_block(args):
        qb, start = args
        s = jnp.einsum('bqhd,bshd->bhqs', qb, k).astype(jnp.float32) * scale
        qpos = start + jnp.arange(Q_BLOCK)
        mask = key_pos[None, :] <= qpos[:, None]
        s = jnp.where(mask[None, None], s, -jnp.inf)
        p = jax.nn.softmax(s, axis=-1).astype(v.dtype)
        return jnp.einsum('bhqs,bshd->bqhd', p, v)

    starts = jnp.arange(t // Q_BLOCK, dtype=jnp.int32) * Q_BLOCK
    return from_blocks(lax.map(one_block, (to_blocks(q), starts)))


def mla_branch(c_q, c_kv, k_rope_raw, g_q, g_kv, w_uq, w_ukv, cos, sin):
    b, t, _ = c_q.shape
    q = (rms_norm(c_q, g_q) @ w_uq).reshape(b, t, MLA_HEADS, MLA_NOPE + MLA_ROPE)
    q = jnp.concatenate([q[..., :MLA_NOPE],
                         apply_rotary(q[..., MLA_NOPE:], cos, sin, MLA_ROPE)], axis=-1)
    kv = (rms_norm(c_kv, g_kv) @ w_ukv).reshape(b, t, MLA_HEADS, MLA_NOPE + MLA_V)
    k_nope, v = kv[..., :MLA_NOPE], kv[..., MLA_NOPE:]
    k_rope = apply_rotary(k_rope_raw, cos, sin, MLA_ROPE)
    k = jnp.concatenate(
        [k_nope, jnp.broadcast_to(k_rope[:, :, None, :], (b, t, MLA_HEADS, MLA_ROPE))], axis=-1)
    o = dense_causal_attention(q, k, v, (MLA_NOPE + MLA_ROPE) ** -0.5)
    return o.reshape(b, t, MLA_HEADS * MLA_V)


def dsa_branch(q, k, v, q_idx, k_idx, w_idx, cos_a, sin_a, cos_i, sin_i):
    b, t, _ = q.shape
    grp = DSA_HEADS // DSA_KV_HEADS
    q = apply_rotary(q.reshape(b, t, DSA_KV_HEADS, grp, DSA_HEAD_DIM), cos_a, sin_a, DSA_ROT)
    k = apply_rotary(k.reshape(b, t, DSA_KV_HEADS, DSA_HEAD_DIM), cos_a, sin_a, DSA_ROT)
    v = v.reshape(b, t, DSA_KV_HEADS, DSA_HEAD_DIM)
    q_idx = apply_rotary(q_idx.reshape(b, t, IDX_HEADS, IDX_HEAD_DIM), cos_i, sin_i, IDX_ROT)
    k_idx = apply_rotary(k_idx, cos_i, sin_i, IDX_ROT)
    w_idx = w_idx.astype(jnp.float32) * (IDX_HEADS * IDX_HEAD_DIM) ** -0.5
    top_k = min(INDEX_TOPK, t // 4)
    key_pos = jnp.arange(t)
    gather = jax.vmap(lambda a, i: a[i])

    def one_block(args):
        qb, qib, wb, start = args
        qpos = start + jnp.arange(Q_BLOCK)
        causal = key_pos[None, :] <= qpos[:, None]
        dots = jnp.einsum('bqhd,bsd->bqhs', qib, k_idx).astype(jnp.float32)
        score = jnp.einsum('bqh,bqhs->bqs', wb, jax.nn.relu(dots))
        score = jnp.where(causal[None], score, -jnp.inf)
        _, idx = lax.top_k(score, top_k)
        valid = idx <= qpos[None, :, None]
        k_sel = gather(k, idx)
        v_sel = gather(v, idx)
        s = jnp.einsum('bqgrd,bqkgd->bqgrk', qb, k_sel).astype(jnp.float32) * DSA_HEAD_DIM ** -0.5
        s = jnp.where(valid[:, :, None, None, :], s, -jnp.inf)
        p = jax.nn.softmax(s, axis=-1).astype(v.dtype)
        return jnp.einsum('bqgrk,bqkgd->bqgrd', p, v_sel)

    starts = jnp.arange(t // Q_BLOCK, dtype=jnp.int32) * Q_BLOCK
    o = lax.map(one_block, (to_blocks(q), to_blocks(q_idx), to_blocks(w_idx), starts))
    return from_blocks(o).reshape(b, t, DSA_HEADS * DSA_HEAD_DIM)


def conv_branch(h, w_dw, b_dw, g_ln, b_ln):
    a, gate = jnp.split(h, 2, axis=-1)
    y = a * jax.nn.sigmoid(gate)
    y = lax.conv_general_dilated(
        y, w_dw[:, None, :].astype(y.dtype), window_strides=(1,),
        padding=[(CONV_WIDTH - 1, 0)],
        dimension_numbers=('NWC', 'WIO', 'NWC'),
        feature_group_count=CONV_CH) + b_dw
    return jax.nn.silu(layer_norm(y, g_ln, b_ln))


def hybrid_mixer(u, w_in, mla_q_norm, mla_kv_norm, mla_w_uq, mla_w_ukv,
                 conv_w_dw, conv_b_dw, conv_ln_g, conv_ln_b,
                 w_o_mla, w_o_dsa, w_o_conv, w_out, ropes):
    b, t, _ = u.shape
    (c_q, c_kv, k_rope, q_a, k_a, v_a, q_i, k_i, w_i, conv_in, gates) = jnp.split(
        u @ w_in, IN_SPLITS, axis=-1)
    cos_m, sin_m, cos_a, sin_a, cos_i, sin_i = ropes
    y_a = mla_branch(c_q, c_kv, k_rope, mla_q_norm, mla_kv_norm, mla_w_uq, mla_w_ukv,
                     cos_m, sin_m) @ w_o_mla
    y_b = dsa_branch(q_a, k_a, v_a, q_i, k_i, w_i, cos_a, sin_a, cos_i, sin_i) @ w_o_dsa
    y_c = conv_branch(conv_in, conv_w_dw, conv_b_dw, conv_ln_g, conv_ln_b) @ w_o_conv
    g = jax.nn.sigmoid(gates.astype(jnp.float32)).astype(u.dtype).reshape(b, t, N_BRANCH, D_MODEL)
    merged = g[:, :, 0] * y_a + g[:, :, 1] * y_b + g[:, :, 2] * y_c
    return merged @ w_out


def squared_relu_mlp(u, w_up, w_down):
    h = jax.nn.relu(u @ w_up)
    return (h * h) @ w_down


def setup_inputs(seed: int = 0) -> dict:
    key = jax.random.key(seed)
    ks = jax.random.split(key, 24)
    f32 = jnp.float32

    def dense(k, shape, fan_in):
        return jax.random.normal(k, shape, f32) * fan_in ** -0.5

    def gain(k, shape):
        return 1.0 + 0.02 * jax.random.normal(k, shape, f32)

    def small(k, shape):
        return 0.02 * jax.random.normal(k, shape, f32)

    L = DEPTH
    return {
        "x": jax.random.normal(ks[0], (BATCH, SEQ, D_MODEL), f32),
        "attn_norm": gain(ks[1], (L, D_MODEL)),
        "w_in": dense(ks[2], (L, D_MODEL, IN_COLS), D_MODEL),
        "mla_q_norm": gain(ks[3], (L, MLA_Q_LORA)),
        "mla_kv_norm": gain(ks[4], (L, MLA_KV_LORA)),
        "mla_w_uq": dense(ks[5], (L, MLA_Q_LORA, MLA_HEADS * (MLA_NOPE + MLA_ROPE)), MLA_Q_LORA),
        "mla_w_ukv": dense(ks[6], (L, MLA_KV_LORA, MLA_HEADS * (MLA_NOPE + MLA_V)), MLA_KV_LORA),
        "conv_w_dw": dense(ks[7], (L, CONV_WIDTH, CONV_CH), CONV_WIDTH),
        "conv_b_dw": small(ks[8], (L, CONV_CH)),
        "conv_ln_g": gain(ks[9], (L, CONV_CH)),
        "conv_ln_b": small(ks[10], (L, CONV_CH)),
        "w_o_mla": dense(ks[11], (L, MLA_HEADS * MLA_V, D_MODEL), MLA_HEADS * MLA_V),
        "w_o_dsa": dense(ks[12], (L, DSA_HEADS * DSA_HEAD_DIM, D_MODEL), DSA_HEADS * DSA_HEAD_DIM),
        "w_o_conv": dense(ks[13], (L, CONV_CH, D_MODEL), CONV_CH),
        "w_out": dense(ks[14], (L, D_MODEL, D_MODEL), D_MODEL),
        "mlp_norm": gain(ks[15], (L, D_MODEL)),
        "w_up": dense(ks[16], (L, D_MODEL, D_FF), D_MODEL),
        "w_down": dense(ks[17], (L, D_FF, D_MODEL), 2 * D_FF),
        "final_norm": gain(ks[18], (D_MODEL,)),
    }


def reference(x, attn_norm, w_in, mla_q_norm, mla_kv_norm, mla_w_uq, mla_w_ukv,
              conv_w_dw, conv_b_dw, conv_ln_g, conv_ln_b,
              w_o_mla, w_o_dsa, w_o_conv, w_out, mlp_norm, w_up, w_down, final_norm):
    t = x.shape[1]
    cos_m, sin_m = rope_tables(t, MLA_ROPE)
    cos_a, sin_a = rope_tables(t, DSA_ROT)
    cos_i, sin_i = rope_tables(t, IDX_ROT)
    ropes = (cos_m, sin_m, cos_a, sin_a, cos_i, sin_i)
    for i in range(DEPTH):
        u = rms_norm(x, attn_norm[i])
        x = x + hybrid_mixer(u, w_in[i], mla_q_norm[i], mla_kv_norm[i], mla_w_uq[i], mla_w_ukv[i],
                             conv_w_dw[i], conv_b_dw[i], conv_ln_g[i], conv_ln_b[i],
                             w_o_mla[i], w_o_dsa[i], w_o_conv[i], w_out[i], ropes)
        x = x + squared_relu_mlp(rms_norm(x, mlp_norm[i]), w_up[i], w_down[i])
    return rms_norm(x, final_norm)
```

```python
import contextlib
import numpy as np
import ml_dtypes
import concourse.bass as bass
import concourse.mybir as mybir
from concourse.bass_utils import run_bass_kernel_spmd

F32 = mybir.dt.float32
BF16 = mybir.dt.bfloat16
ALU = mybir.AluOpType
AF = mybir.ActivationFunctionType
AX = mybir.AxisListType
NPBF = ml_dtypes.bfloat16

D = 2048
SEQ = 4096
NB = 4
NCORE = 8
NTOK = 2048
TT = 512
NT = 4
KC = 16
DFF = 8192
EPS = 1e-6
THETA = 500000.0
IN_COLS = 15504
C_CQ, C_CKV, C_KR, C_QD, C_KD, C_VD, C_QI, C_KI, C_WI, C_GA, C_GG, C_GT = (
    0, 512, 1024, 1088, 3136, 3648, 4160, 5184, 5248, 5264, 7312, 9360)
BLOCKS = ((0, 3, 4, 7), (1, 2, 5, 6))
STQ = "act"

ENGS = ("pe", "act", "dve", "pool", "sp")


class Buf:
    __slots__ = ("name", "w", "r")

    def __init__(self, name):
        self.name = name
        self.w = None
        self.r = []


class Op:
    __slots__ = ("eng", "fn", "deps", "dwaits", "sig", "dsem", "dval", "idx")


class Prog:
    def __init__(self, nc):
        self.nc = nc
        self.ops = []
        self.dma_streams = {}
        self.dbufs = {}

    def dbuf(self, *key):
        b = self.dbufs.get(key)
        if b is None:
            b = Buf(str(key))
            self.dbufs[key] = b
        return b

    def _record(self, eng, fn, reads, writes, dsem=None):
        op = Op()
        op.eng = eng
        op.fn = fn
        op.sig = dsem is not None
        op.dsem = dsem
        op.dval = 0
        op.idx = len(self.ops)
        deps = set()
        for b in reads:
            if b.w is not None:
                deps.add(b.w)
        for b in writes:
            if b.w is not None:
                deps.add(b.w)
            for r in b.r:
                deps.add(r)
        cdeps = set()
        dwaits = {}
        for d in deps:
            p = self.ops[d]
            if p.dsem is not None:
                dwaits[p.dsem] = self.dma_streams[p.dsem][0]
            elif not (eng == "pe" and dsem is None and p.eng == "pe"):
                cdeps.add(d)
        op.deps = cdeps
        op.dwaits = dwaits
        if dsem is not None:
            c = self.dma_streams.setdefault(dsem, [0])
            c[0] += 16
            op.dval = c[0]
        self.ops.append(op)
        for b in reads:
            b.r.append(op.idx)
        for b in writes:
            b.w = op.idx
            b.r = []
        return op

    def op(self, eng, fn, reads=(), writes=()):
        return self._record(eng, fn, list(reads), list(writes))

    def dma(self, queue, stream, out, in_, reads=(), writes=()):
        def fn(e, out=out, in_=in_):
            return e.dma_start(out=out, in_=in_)
        return self._record(queue, fn, list(reads), list(writes), dsem=stream)

    def custom_dma(self, queue, stream, fn, reads=(), writes=()):
        return self._record(queue, fn, list(reads), list(writes), dsem=stream)

    def barrier(self):
        last = {}
        for op in self.ops:
            if op.dsem is None:
                last[op.eng] = op.idx
        for e in ENGS:
            op = self._record(e, lambda eng: eng.nop(), [], [])
            for e2, i in last.items():
                if e2 != e:
                    op.deps.add(i)
            for k, c in self.dma_streams.items():
                op.dwaits[k] = c[0]

    def emit(self, final_wait_eng="sp"):
        nc = self.nc
        ops = self.ops
        for op in ops:
            for d in op.deps:
                ops[d].sig = True
        cnt = {e: 0 for e in ENGS}
        for op in ops:
            if op.dsem is None and op.sig:
                cnt[op.eng] += 1
                op.dval = cnt[op.eng]
        with contextlib.ExitStack() as st:
            esem = {e: st.enter_context(nc.semaphore("s_" + e)) for e in ENGS}
            dsem = {k: st.enter_context(nc.semaphore("d_" + k)) for k in self.dma_streams}
            block = st.enter_context(nc.Block())
            per_eng = {e: [] for e in ENGS}
            for op in ops:
                per_eng[op.eng].append(op)

            def run(eng_name, eobj):
                waited = {}
                for op in per_eng[eng_name]:
                    need = {}
                    for d in op.deps:
                        p = ops[d]
                        key = ("e", p.eng)
                        if p.dval > need.get(key, 0):
                            need[key] = p.dval
                    for k, v in op.dwaits.items():
                        need[("d", k)] = v
                    for key, v in need.items():
                        if waited.get(key, 0) >= v:
                            continue
                        waited[key] = v
                        s = dsem[key[1]] if key[0] == "d" else esem[key[1]]
                        eobj.wait_ge(s, v)
                    ins = op.fn(eobj)
                    if op.dsem is not None:
                        ins.then_inc(dsem[op.dsem], 16)
                    elif op.sig:
                        ins.then_inc(esem[op.eng], 1)
                if eng_name == final_wait_eng:
                    for k, c in self.dma_streams.items():
                        eobj.wait_ge(dsem[k], c[0])
                    for e in ENGS:
                        if cnt[e] > 0 and e != eng_name:
                            eobj.wait_ge(esem[e], cnt[e])

            @block.tensor
            def _(e):
                run("pe", e)

            @block.scalar
            def _(e):
                run("act", e)

            @block.vector
            def _(e):
                run("dve", e)

            @block.gpsimd
            def _(e):
                run("pool", e)

            @block.sync
            def _(e):
                run("sp", e)


DSIZE = {F32: 4, BF16: 2}


class Arena:
    def __init__(self, nc, nbytes=204 * 1024):
        self.h = nc.alloc_sbuf_tensor("arena", [128, nbytes], mybir.dt.uint8)
        self.cap = nbytes
        self.off = 0

    def reset(self):
        self.off = 0

    def alloc(self, shape, dtype):
        n = 1
        for d in shape[1:]:
            n *= d
        nb = (n * DSIZE[dtype] + 31) // 32 * 32
        assert self.off + nb <= self.cap, "arena overflow %d + %d" % (self.off, nb)
        ap = self.h[0:shape[0], self.off:self.off + n * DSIZE[dtype]].bitcast(dtype)
        self.off += nb
        if len(shape) == 3:
            ap = ap.rearrange("p (a b) -> p a b", a=shape[1])
        elif len(shape) == 4:
            ap = ap.rearrange("p (a b c) -> p a b c", a=shape[1], b=shape[2])
        return ap


class Ctx:
    def __init__(self, nc):
        self.nc = nc
        self.P = Prog(nc)
        self.arena = Arena(nc)
        self.ps = Ring(self, "ps", 8, [128, 512], F32, psum=True)


class T:
    def __init__(self, cx, name, shape, dtype, psum=False, buf=None):
        if psum:
            self.ap = cx.nc.alloc_psum_tensor(name, shape, dtype)[:]
        else:
            self.ap = cx.arena.alloc(shape, dtype)
        self.b = buf if buf is not None else Buf(name)
        self.name = name
        self.slot = 0

    def __getitem__(self, k):
        return self.ap[k]


class Ring:
    def __init__(self, cx, name, n, shape, dtype, psum=False):
        self.ts = [T(cx, "%s%d" % (name, i), shape, dtype, psum=psum) for i in range(n)]
        self.i = 0
        self.name = name

    def next(self):
        t = self.ts[self.i % len(self.ts)]
        t.slot = self.i % len(self.ts)
        self.i += 1
        return t


def sl(i, n):
    return slice(i * n, (i + 1) * n)


def phase_a(cx, io):
    nc, P, ps = cx.nc, cx.P, cx.ps
    uT = T(cx, "uT", [128, KC, NTOK], BF16)
    cqn = T(cx, "cqn", [128, 4, NTOK], BF16)
    ones = T(cx, "onesA", [128, 128], BF16)
    gA = T(cx, "gA", [128, KC], F32)
    gq = T(cx, "gq", [128, 4], F32)
    gkv = T(cx, "gkv", [128, 4], F32)
    wst = Ring(cx, "wst", 2, [128, KC, 256], F32)
    wb = Ring(cx, "wb", 2, [128, KC, 256], BF16)
    wsw = Ring(cx, "wsw", 2, [128, KC, 256], BF16)
    tabs = Ring(cx, "tabs", 4, [128, 512], F32)
    tmp = Ring(cx, "tmpA", 6, [128, 512], F32)
    outs = Ring(cx, "outA", 6, [128, 512], F32)
    rstd = T(cx, "rstdA", [128, 512], F32)
    sm = T(cx, "smA", [128, 512], F32)

    P.op("pool", lambda e: e.memset(ones[:], 1.0), writes=[ones.b])
    P.dma("sp", "c0", gA[:], io["attn_norm"][:, :], writes=[gA.b])
    P.dma("sp", "c1", gq[:], io["q_norm"][:, :], writes=[gq.b])
    P.dma("sp", "c2", gkv[:], io["kv_norm"][:, :], writes=[gkv.b])

    def rstd_from(ssbank, n_feat):
        P.op("act", lambda e: e.activation(out=sm[:], in_=ssbank[:], func=AF.Sqrt,
                                           scale=1.0 / n_feat, bias=EPS),
             reads=[ssbank.b], writes=[sm.b])
        P.op("dve", lambda e: e.reciprocal(out=rstd[:], in_=sm[:]), reads=[sm.b], writes=[rstd.b])

    xv = io["xT"].rearrange("(kc p) n -> p kc n", p=128)
    for tt in range(NT):
        xh, qh = [], []
        for hf in range(2):
            ws = wst.next()
            xs = ws[:].rearrange("p a b -> p (a b)").rearrange("p (kc n) -> p kc n", kc=8)
            P.dma("sp", "w%d" % ws.slot, xs, xv[:, sl(hf, 8), sl(tt, TT)], writes=[ws.b])
            wq = wb.next()
            qs = wq[:].rearrange("p a b -> p (a b)").rearrange("p (kc n) -> p kc n", kc=8)
            P.op("act", lambda e, qs=qs, xs=xs: e.activation(out=qs, in_=xs, func=AF.Square),
                 reads=[ws.b], writes=[wq.b])
            xh.append((ws, xs))
            qh.append((wq, qs))
        ss = ps.next()
        for kc in range(KC):
            wq, qs = qh[kc // 8]
            P.op("pe", lambda e, kc=kc, ss=ss, qs=qs: e.matmul(ss[:], lhsT=ones[:], rhs=qs[:, kc % 8, :],
                                                               start=(kc == 0), stop=(kc == KC - 1)),
                 reads=[ones.b, wq.b], writes=[ss.b])
        rstd_from(ss, D)
        for kc in range(KC):
            eng = "dve"
            ws, xs = xh[kc // 8]
            P.op(eng, lambda e, kc=kc, tt=tt, xs=xs: e.scalar_tensor_tensor(
                out=uT[:, kc, sl(tt, TT)], in0=xs[:, kc % 8, :], scalar=gA[:, kc:kc + 1],
                in1=rstd[:], op0=ALU.mult, op1=ALU.mult),
                reads=[ws.b, gA.b, rstd.b], writes=[uT.b])

    def load_w(src_view, segs, kcn=KC, rot=None):
        ws = wst.next()
        o = 0
        for (c0, n) in segs:
            P.dma("sp", "w%d" % ws.slot, ws[:, 0:kcn, o:o + n], src_view[:, :, c0:c0 + n], writes=[ws.b])
            o += n
        w = wb.next()
        P.op("pool", lambda e: e.tensor_copy(out=w[:, 0:kcn, 0:o], in_=ws[:, 0:kcn, 0:o]),
             reads=[ws.b], writes=[w.b])
        sw = None
        if rot is not None:
            sw = wsw.next()
            P.op("pool", lambda e: e.memset(sw[:, 0:kcn, 0:o], 0.0), writes=[sw.b])
            for (r0, hf) in rot:
                P.op("pool", lambda e, r0=r0, hf=hf: e.tensor_scalar(
                    out=sw[:, 0:kcn, r0:r0 + hf], in0=ws[:, 0:kcn, r0 + hf:r0 + 2 * hf],
                    scalar1=-1.0, scalar2=None, op0=ALU.mult), reads=[ws.b], writes=[sw.b])
                P.op("pool", lambda e, r0=r0, hf=hf: e.tensor_copy(
                    out=sw[:, 0:kcn, r0 + hf:r0 + 2 * hf], in_=ws[:, 0:kcn, r0:r0 + hf]),
                    reads=[ws.b], writes=[sw.b])
        return w, sw

    def proj(w, m0, M, src, kcn, tt):
        bank = ps.next()
        for kc in range(kcn):
            P.op("pe", lambda e, kc=kc, bank=bank: e.matmul(
                bank[0:M, :], lhsT=w[:, kc, m0:m0 + M], rhs=src[:, kc, sl(tt, TT)],
                start=(kc == 0), stop=(kc == kcn - 1)), reads=[w.b, src.b], writes=[bank.b])
        return bank

    def store(dst_ap, t, ap, key):
        P.dma(STQ, "o%s%d" % (t.name[:4], t.slot), dst_ap, ap, reads=[t.b], writes=[P.dbuf(*key)])

    def load_tab(tab_ap, rows, tt):
        t = tabs.next()
        P.dma("sp", "tb%d" % t.slot, t[0:rows, :], tab_ap[:, sl(tt, TT)], writes=[t.b])
        return t

    def rope_epilogue(pre, swp, M, tabC, tabS, dst_ap, key):
        t1 = tmp.next()
        P.op("act", lambda e: e.activation(out=t1[0:M, :], in_=swp[0:M, :], func=AF.Copy),
             reads=[swp.b], writes=[t1.b])
        t2 = tmp.next()
        P.op("pool", lambda e: e.tensor_tensor(out=t2[0:M, :], in0=t1[0:M, :], in1=tabS[0:M, :], op=ALU.mult),
             reads=[t1.b, tabS.b], writes=[t2.b])
        t3 = tmp.next()
        P.op("dve", lambda e: e.tensor_tensor(out=t3[0:M, :], in0=pre[0:M, :], in1=tabC[0:M, :], op=ALU.mult),
             reads=[pre.b, tabC.b], writes=[t3.b])
        o = outs.next()
        ob = o[:].bitcast(BF16)
        P.op("pool", lambda e: e.tensor_tensor(out=ob[0:M, 0:512], in0=t3[0:M, :], in1=t2[0:M, :], op=ALU.add),
             reads=[t3.b, t2.b], writes=[o.b])
        store(dst_ap, o, ob[0:M, 0:512], key)

    def copy_epilogue(pre, M, dst_ap, key, func=AF.Copy):
        o = outs.next()
        ob = o[:].bitcast(BF16)
        P.op("act", lambda e: e.activation(out=ob[0:M, 0:512], in_=pre[0:M, :], func=func),
             reads=[pre.b], writes=[o.b])
        store(dst_ap, o, ob[0:M, 0:512], key)

    wv = io["w_in"].rearrange("(kc p) c -> p kc c", p=128)

    def latent(c0, g, dst_sb, dst_dram, nm):
        wts = [load_w(wv, [(c0 + 256 * i, 256)])[0] for i in range(2)]
        for tt in range(NT):
            banks = []
            ssb = None
            for j in range(4):
                bk = proj(wts[j // 2], (j % 2) * 128, 128, uT, KC, tt)
                banks.append(bk)
                sq = tmp.next()
                sqb = sq[:].bitcast(BF16)
                P.op("act", lambda e, bk=bk, sqb=sqb: e.activation(out=sqb[:, 0:512], in_=bk[:], func=AF.Square),
                     reads=[bk.b], writes=[sq.b])
                if ssb is None:
                    ssb = ps.next()
                P.op("pe", lambda e, j=j, sqb=sqb, ssb=ssb: e.matmul(ssb[:], lhsT=ones[:], rhs=sqb[:, 0:512],
                                                                      start=(j == 0), stop=(j == 3)),
                     reads=[ones.b, sq.b], writes=[ssb.b])
            rstd_from(ssb, 512)
            for j in range(4):
                bk = banks[j]
                if dst_sb is not None:
                    P.op("dve", lambda e, j=j, bk=bk, tt=tt: e.scalar_tensor_tensor(
                        out=dst_sb[:, j, sl(tt, TT)], in0=bk[:], scalar=g[:, j:j + 1], in1=rstd[:],
                        op0=ALU.mult, op1=ALU.mult), reads=[bk.b, g.b, rstd.b], writes=[dst_sb.b])
                else:
                    o = outs.next()
                    ob = o[:].bitcast(BF16)
                    P.op("dve", lambda e, j=j, bk=bk, ob=ob: e.scalar_tensor_tensor(
                        out=ob[:, 0:512], in0=bk[:], scalar=g[:, j:j + 1], in1=rstd[:],
                        op0=ALU.mult, op1=ALU.mult), reads=[bk.b, g.b, rstd.b], writes=[o.b])
                    store(dst_dram[sl(j, 128), sl(tt, TT)], o, ob[:, 0:512], (nm, j, tt))

    latent(C_CQ, gq, cqn, None, "cq")
    latent(C_CKV, gkv, None, io["ckvn"], "ckvn")

    wuq = io["w_uq"].rearrange("(kc p) c -> p kc c", p=128)
    for h in range(16):
        w, sw = load_w(wuq, [(h * 192, 192)], kcn=4, rot=[(128, 32)])
        for tt in range(NT):
            bk = proj(w, 0, 128, cqn, 4, tt)
            copy_epilogue(bk, 128, io["qm"][h, 0:128, sl(tt, TT)], ("qm", h, 0, tt))
            pre = proj(w, 128, 64, cqn, 4, tt)
            swp = proj(sw, 128, 64, cqn, 4, tt)
            tc_ = load_tab(io["tabM_C"], 64, tt)
            ts_ = load_tab(io["tabM_S"], 64, tt)
            rope_epilogue(pre, swp, 64, tc_, ts_, io["qm"][h, 128:192, sl(tt, TT)], ("qm", h, 1, tt))

    w, sw = load_w(wv, [(C_KR, 64)], rot=[(0, 32)])
    for tt in range(NT):
        pre = proj(w, 0, 64, uT, KC, tt)
        swp = proj(sw, 0, 64, uT, KC, tt)
        rope_epilogue(pre, swp, 64, load_tab(io["tabM_C"], 64, tt), load_tab(io["tabM_S"], 64, tt),
                      io["krope"][:, sl(tt, TT)], ("krope", tt))

    for (c0, nheads, dst, nm) in ((C_QD, 16, io["qd"], "qd"), (C_KD, 4, io["kd"], "kd")):
        for i in range(nheads // 2):
            w, sw = load_w(wv, [(c0 + 256 * i, 256)], rot=[(0, 16), (128, 16)])
            for tt in range(NT):
                tc_ = load_tab(io["tabA_C"], 128, tt)
                ts_ = load_tab(io["tabA_S"], 128, tt)
                for j in range(2):
                    pre = proj(w, j * 128, 128, uT, KC, tt)
                    swp = proj(sw, j * 128, 128, uT, KC, tt)
                    rope_epilogue(pre, swp, 128, tc_, ts_, dst[2 * i + j, :, sl(tt, TT)], (nm, 2 * i + j, tt))

    for i in range(4):
        w, sw = load_w(wv, [(C_QI + 256 * i, 256)], rot=[(0, 8), (64, 8), (128, 8), (192, 8)])
        for tt in range(NT):
            tc_ = load_tab(io["tabI_C"], 128, tt)
            ts_ = load_tab(io["tabI_S"], 128, tt)
            for j in range(2):
                pre = proj(w, j * 128, 128, uT, KC, tt)
                swp = proj(sw, j * 128, 128, uT, KC, tt)
                rope_epilogue(pre, swp, 128, tc_, ts_, io["qi"][2 * i + j, :, sl(tt, TT)], ("qi", 2 * i + j, tt))
    w, sw = load_w(wv, [(C_KI, 64)], rot=[(0, 8)])
    for tt in range(NT):
        pre = proj(w, 0, 64, uT, KC, tt)
        swp = proj(sw, 0, 64, uT, KC, tt)
        rope_epilogue(pre, swp, 64, load_tab(io["tabI_C"], 128, tt), load_tab(io["tabI_S"], 128, tt),
                      io["ki"][:, sl(tt, TT)], ("ki", tt))

    for i in range(2):
        w, _ = load_w(wv, [(C_VD + 256 * i, 256)])
        for tb in range(NTOK // 128):
            bank = ps.next()
            for kc in range(KC):
                P.op("pe", lambda e, kc=kc, bank=bank, tb=tb, w=w: e.matmul(
                    bank[:, 0:256], lhsT=uT[:, kc, sl(tb, 128)], rhs=w[:, kc, 0:256],
                    start=(kc == 0), stop=(kc == KC - 1)), reads=[w.b, uT.b], writes=[bank.b])
            o = outs.next()
            ob = o[:].bitcast(BF16)
            P.op("act", lambda e, bank=bank, ob=ob: e.activation(out=ob[:, 0:256], in_=bank[:, 0:256], func=AF.Copy),
                 reads=[bank.b], writes=[o.b])
            store(io["vd"][sl(tb, 128), sl(i, 256)], o, ob[:, 0:256], ("vd", tb, i))
    w, _ = load_w(wv, [(C_WI, 16)])
    for tb in range(NTOK // 128):
        bank = ps.next()
        for kc in range(KC):
            P.op("pe", lambda e, kc=kc, bank=bank, tb=tb, w=w: e.matmul(
                bank[:, 0:16], lhsT=uT[:, kc, sl(tb, 128)], rhs=w[:, kc, 0:16],
                start=(kc == 0), stop=(kc == KC - 1)), reads=[w.b, uT.b], writes=[bank.b])
        o = outs.next()
        P.op("act", lambda e, bank=bank, o=o: e.activation(out=o[:, 0:16], in_=bank[:, 0:16], func=AF.Copy,
                                                           scale=float(1024.0 ** -0.5)),
             reads=[bank.b], writes=[o.b])
        store(io["wi"][sl(tb, 128), :], o, o[:, 0:16], ("wi", tb))

    for j in range(16):
        w, _ = load_w(wv, [(C_GA + 128 * j, 128), (C_GG + 128 * j, 128)])
        for tt in range(NT):
            pa = proj(w, 0, 128, uT, KC, tt)
            pg = proj(w, 128, 128, uT, KC, tt)
            sg = tmp.next()
            P.op("act", lambda e, pg=pg, sg=sg: e.activation(out=sg[:], in_=pg[:], func=AF.Sigmoid),
                 reads=[pg.b], writes=[sg.b])
            o = outs.next()
            P.op("dve", lambda e, pa=pa, sg=sg, o=o: e.tensor_tensor(out=o[:], in0=pa[:], in1=sg[:], op=ALU.mult),
                 reads=[pa.b, sg.b], writes=[o.b])
            store(io["glu"][sl(j, 128), sl(tt, TT)], o, o[:], ("glu", j, tt))

    for i in range(24):
        w, _ = load_w(wv, [(C_GT + 256 * i, 256)])
        for tt in range(NT):
            for j in range(2):
                bk = proj(w, j * 128, 128, uT, KC, tt)
                copy_epilogue(bk, 128, io["gates"][sl(2 * i + j, 128), sl(tt, TT)], ("gates", 2 * i + j, tt),
                              func=AF.Sigmoid)


A_OUT = {
    "qm": ([16, 192, NTOK], BF16), "ckvn": ([512, NTOK], BF16), "krope": ([64, NTOK], BF16),
    "qd": ([16, 128, NTOK], BF16), "kd": ([4, 128, NTOK], BF16), "vd": ([NTOK, 512], BF16),
    "qi": ([8, 128, NTOK], BF16), "ki": ([64, NTOK], BF16), "wi": ([NTOK, 16], F32),
    "glu": ([D, NTOK], F32), "gates": ([3 * D, NTOK], BF16),
}
A_IN = {
    "xT": ([D, NTOK], F32), "w_in": ([D, IN_COLS], F32), "attn_norm": ([128, KC], F32),
    "q_norm": ([128, 4], F32), "kv_norm": ([128, 4], F32), "w_uq": ([512, 3072], F32),
    "tabM_C": ([64, NTOK], F32), "tabM_S": ([64, NTOK], F32),
    "tabA_C": ([128, NTOK], F32), "tabA_S": ([128, NTOK], F32),
    "tabI_C": ([128, NTOK], F32), "tabI_S": ([128, NTOK], F32),
}


def build_phase(phase_fn, ins, outs_):
    nc = bass.Bass("TRN2", target_bir_lowering=False)
    io = {}
    for k, (shp, dt) in ins.items():
        io[k] = nc.dram_tensor(k, shp, dt, kind="ExternalInput").ap()
    for k, (shp, dt) in outs_.items():
        io[k] = nc.dram_tensor(k, shp, dt, kind="ExternalOutput").ap()
    cx = Ctx(nc)
    phase_fn(cx, io)
    cx.P.emit()
    return nc


def core_tokens(c):
    half = c % 2
    idx = np.concatenate([np.arange(g * TT, (g + 1) * TT) for g in BLOCKS[half]])
    return c // 2, idx


def vec_layout(v):
    n = v.shape[0] // 128
    return np.ascontiguousarray(v.reshape(n, 128).T)


def rope_tables(pos):
    def tab(rot):
        inv = (np.float32(THETA) ** (-np.arange(0, rot, 2, dtype=np.float32) / np.float32(rot))).astype(np.float32)
        ang = pos.astype(np.float32)[:, None] * inv[None, :]
        return np.cos(ang).astype(np.float32).T, np.sin(ang).astype(np.float32).T
    n = pos.shape[0]
    cm, sm_ = tab(64)
    ca, sa = tab(32)
    ci, si = tab(16)
    one = np.ones
    zero = np.zeros
    tM_C = np.concatenate([cm, cm], 0)
    tM_S = np.concatenate([sm_, sm_], 0)
    tA_C = np.concatenate([ca, ca, one((96, n), np.float32)], 0)
    tA_S = np.concatenate([sa, sa, zero((96, n), np.float32)], 0)
    hC = np.concatenate([ci, ci, one((48, n), np.float32)], 0)
    hS = np.concatenate([si, si, zero((48, n), np.float32)], 0)
    tI_C = np.concatenate([hC, hC], 0)
    tI_S = np.concatenate([hS, hS], 0)
    return dict(tabM_C=tM_C, tabM_S=tM_S, tabA_C=tA_C, tabA_S=tA_S, tabI_C=tI_C, tabI_S=tI_S)


KMAX = (2, 4, 6, 8)
NIT = 20
TOPK = 256


class SubRing:
    def __init__(self, ts):
        self.ts = ts
        self.i = 0

    def next(self):
        t = self.ts[self.i % len(self.ts)]
        t.slot = self.i % len(self.ts)
        self.i += 1
        return t


def attn_core(cx, nkb, lhs_fn, pt_ring, st_ring, acc_ring, mask_fn, vfn, ones, scale, fin):
    P = cx.P
    OT = acc_ring.next()
    LT = acc_ring.next()
    for jb in range(nkb):
        ST = st_ring.next()
        lhs_fn(jb, ST)
        PT = pt_ring.next()
        P.op("act", lambda e, ST=ST, PT=PT: e.activation(out=PT[:], in_=ST[:], func=AF.Exp, scale=scale),
             reads=[ST.b], writes=[PT.b])
        m = mask_fn(jb)
        if m is not None:
            mt, map_ = m
            P.op("pool", lambda e, PT=PT, map_=map_: e.tensor_tensor(out=PT[:], in0=PT[:], in1=map_, op=ALU.mult),
                 reads=[PT.b, mt.b], writes=[PT.b])
        vt, vap = vfn(jb)
        P.op("pe", lambda e, PT=PT, vap=vap, jb=jb: e.matmul(OT[:], lhsT=vap, rhs=PT[:], start=(jb == 0),
                                                             stop=(jb == nkb - 1)),
             reads=[vt.b, PT.b], writes=[OT.b])
        P.op("pe", lambda e, PT=PT, jb=jb: e.matmul(LT[:], lhsT=ones[:], rhs=PT[:], start=(jb == 0),
                                                    stop=(jb == nkb - 1)),
             reads=[ones.b, PT.b], writes=[LT.b])
    fin(OT, LT)


def phase_b_mla(cx, io, ks):
    nc, P = cx.nc, cx.P
    cx.arena.reset()
    st_ring = SubRing(cx.ps.ts[0:3])
    acc_ring = SubRing(cx.ps.ts[3:7])
    misc = SubRing(cx.ps.ts[7:8])
    scale = float(192.0 ** -0.5)
    ckv = T(cx, "ckv", [128, 4, SEQ], BF16)
    kr = T(cx, "kr", [64, SEQ], BF16)
    ones = T(cx, "onesB", [128, 128], BF16)
    cm = T(cx, "cm", [128, 32, 512], BF16)
    wst = Ring(cx, "wstB", 2, [128, 4, 256], F32)
    wkb = Ring(cx, "wkb", 2, [128, 4, 256], BF16)
    KT = Ring(cx, "KT", 2, [128, SEQ], BF16)
    V = Ring(cx, "V", 2, [128, 32, 128], BF16)
    qn = Ring(cx, "qn", 2, [128, NTOK], BF16)
    qr = Ring(cx, "qr", 2, [64, NTOK], BF16)
    pt = Ring(cx, "ptB", 4, [128, 512], BF16)
    rl = Ring(cx, "rlB", 2, [128, 512], F32)
    ob = Ring(cx, "obB", 3, [128, 512], BF16)

    P.op("pool", lambda e: e.memset(ones[:], 1.0), writes=[ones.b])
    for gb in range(8):
        P.dma("sp", "bk0", ckv[:, :, sl(gb, TT)], ks["ckvn"](gb).rearrange("(kc p) n -> p kc n", p=128),
              reads=[P.dbuf("x_ckvn")], writes=[ckv.b])
        P.dma("sp", "bk1", kr[:, sl(gb, TT)], ks["krope"](gb), reads=[P.dbuf("x_krope")], writes=[kr.b])
    P.dma("sp", "bk2", cm[:], io["cmask"].rearrange("i j p q -> p (i j) q"), writes=[cm.b])
    wv = io["w_ukv"].rearrange("(kc p) c -> p kc c", p=128)
    for h in range(16):
        ws = wst.next()
        P.dma("sp", "bw%d" % ws.slot, ws[:], wv[:, :, sl(h, 256)], writes=[ws.b])
        w = wkb.next()
        P.op("pool", lambda e, w=w, ws=ws: e.tensor_copy(out=w[:], in_=ws[:]), reads=[ws.b], writes=[w.b])
        qnt = qn.next()
        qrt = qr.next()
        P.dma("sp", "bq%d" % qnt.slot, qnt[:], io["qm"][h, 0:128, :], reads=[P.dbuf("qm")], writes=[qnt.b])
        P.dma("sp", "bq%d" % qnt.slot, qrt[:], io["qm"][h, 128:192, :], reads=[P.dbuf("qm")], writes=[qrt.b])
        kt = KT.next()
        vt = V.next()
        for s8 in range(8):
            bank = misc.next()
            for kc in range(4):
                P.op("pe", lambda e, kc=kc, bank=bank, s8=s8, w=w: e.matmul(
                    bank[:], lhsT=w[:, kc, 0:128], rhs=ckv[:, kc, sl(s8, TT)], start=(kc == 0), stop=(kc == 3)),
                    reads=[w.b, ckv.b], writes=[bank.b])
            P.op("dve", lambda e, bank=bank, s8=s8, kt=kt: e.tensor_copy(out=kt[:, sl(s8, TT)], in_=bank[:]),
                 reads=[bank.b], writes=[kt.b])
        for g4 in range(8):
            bank = misc.next()
            for j in range(4):
                for kc in range(4):
                    P.op("pe", lambda e, kc=kc, bank=bank, j=j, g4=g4, w=w: e.matmul(
                        bank[:, sl(j, 128)], lhsT=ckv[:, kc, sl(g4 * 4 + j, 128)], rhs=w[:, kc, 128:256],
                        start=(kc == 0), stop=(kc == 3)), reads=[w.b, ckv.b], writes=[bank.b])
            P.op("dve", lambda e, bank=bank, g4=g4, vt=vt: e.tensor_copy(
                out=vt[:, g4 * 4:(g4 + 1) * 4, :], in_=bank[:].rearrange("p (a b) -> p a b", a=4)),
                reads=[bank.b], writes=[vt.b])
        for i in range(4):
            nkb = 4 * KMAX[i]

            def lhs_fn(jb, ST, kt=kt, qnt=qnt, qrt=qrt, i=i):
                P.op("pe", lambda e: e.matmul(ST[:], lhsT=kt[:, sl(jb, 128)], rhs=qnt[:, sl(i, TT)],
                                              start=True, stop=False), reads=[kt.b, qnt.b], writes=[ST.b])
                P.op("pe", lambda e: e.matmul(ST[:], lhsT=kr[:, sl(jb, 128)], rhs=qrt[:, sl(i, TT)],
                                              start=False, stop=True), reads=[kr.b, qrt.b], writes=[ST.b])

            def mask_fn(jb, i=i, nkb=nkb):
                if jb >= nkb - 8:
                    return cm, cm[:, i * 8 + jb - (nkb - 8), :]
                return None

            def vfn(jb, vt=vt):
                return vt, vt[:, jb, :]

            def fin(OT, LT, h=h, i=i):
                r = rl.next()
                P.op("dve", lambda e: e.reciprocal(out=r[:], in_=LT[:]), reads=[LT.b], writes=[r.b])
                o = ob.next()
                P.op("dve", lambda e: e.tensor_tensor(out=o[:], in0=OT[:], in1=r[:], op=ALU.mult),
                     reads=[OT.b, r.b], writes=[o.b])
                P.dma(STQ, "bo%d" % o.slot, io["oT_mla"][sl(h, 128), sl(i, TT)], o[:], reads=[o.b],
                      writes=[P.dbuf("oT_mla", h, i)])

            attn_core(cx, nkb, lhs_fn, pt, st_ring, acc_ring, mask_fn, vfn, ones, scale, fin)


def phase_b_dsa(cx, io, ks):
    nc, P = cx.nc, cx.P
    cx.arena.reset()
    st_ring = SubRing(cx.ps.ts[0:3])
    acc_ring = SubRing(cx.ps.ts[3:7])
    misc = SubRing(cx.ps.ts[7:8])
    scale = float(128.0 ** -0.5)
    ki2 = T(cx, "ki2", [128, SEQ], BF16)
    ones = T(cx, "onesD", [128, 128], BF16)
    ident = T(cx, "ident", [128, 128], BF16)
    score = [T(cx, "score%d" % j, [128, SEQ], F32) for j in range(4)]
    for s_ in score:
        s_.bs = [Buf("%s_%d" % (s_.name, k)) for k in range(8)]
    junk = cx.arena.alloc([128, SEQ], BF16)
    Mt = Ring(cx, "Mt", 2, [128, SEQ], BF16)
    MT = T(cx, "MT", [128, 32, 512], BF16)
    KTg = Ring(cx, "KTg", 2, [128, SEQ], BF16)
    Vg = Ring(cx, "Vg", 2, [128, 32, 128], BF16)
    pt = Ring(cx, "ptD", 4, [128, 512], BF16)
    rel = Ring(cx, "relD", 4, [128, 512], F32)
    qt = Ring(cx, "qtD", 2, [128, 512], BF16)
    qit = [T(cx, "qit%d" % j, [128, 8, 128], BF16) for j in range(4)]
    wit = [T(cx, "wit%d" % j, [128, 16], F32) for j in range(4)]
    am = Ring(cx, "am", 4, [128, 512], BF16)
    sm = [T(cx, "smD%d" % j, [128, 8], F32) for j in range(4)]
    rl = Ring(cx, "rlD", 2, [128, 512], F32)
    ob = Ring(cx, "obD", 3, [128, 512], BF16)

    P.op("pool", lambda e: e.memset(ones[:], 1.0), writes=[ones.b])
    P.dma("sp", "dk0", ident[:], io["ident"][:, :], writes=[ident.b])
    for gb in range(8):
        for hf in range(2):
            P.dma("sp", "dk1", ki2[sl(hf, 64), sl(gb, TT)], ks["ki"](gb), reads=[P.dbuf("x_ki")], writes=[ki2.b])

    for i in range(4):
        nkt = KMAX[i]
        nkb = 4 * nkt
        S_c = nkt * TT
        for j in range(4):
            tok = slice(i * TT + j * 128, i * TT + (j + 1) * 128)
            P.dma("sp", "dq%d" % j, qit[j][:], io["qi"][:, :, tok].rearrange("a p n -> p a n"),
                  reads=[P.dbuf("qi")], writes=[qit[j].b])
            P.dma("sp", "dq%d" % j, wit[j][:], io["wi"][tok, :], reads=[P.dbuf("wi")], writes=[wit[j].b])
            for hh in range(16):
                pr, hf = hh // 2, hh % 2
                for kt_ in range(nkt):
                    dots = st_ring.next()
                    P.op("pe", lambda e, dots=dots, j=j, pr=pr, hf=hf, kt_=kt_: e.matmul(
                        dots[:], lhsT=qit[j][sl(hf, 64), pr, :], rhs=ki2[sl(hf, 64), sl(kt_, TT)],
                        start=True, stop=True), reads=[qit[j].b, ki2.b], writes=[dots.b])
                    r = rel.next()
                    P.op("act", lambda e, dots=dots, r=r: e.activation(out=r[:], in_=dots[:], func=AF.Relu),
                         reads=[dots.b], writes=[r.b])
                    sc = score[j]
                    if hh == 0:
                        P.op("dve", lambda e, r=r, sc=sc, kt_=kt_, j=j: e.tensor_scalar(
                            out=sc[:, sl(kt_, TT)], in0=r[:], scalar1=wit[j][:, 0:1], scalar2=None, op0=ALU.mult),
                            reads=[r.b, wit[j].b], writes=[sc.bs[kt_]])
                    else:
                        P.op("dve", lambda e, r=r, sc=sc, kt_=kt_, j=j, hh=hh: e.scalar_tensor_tensor(
                            out=sc[:, sl(kt_, TT)], in0=r[:], scalar=wit[j][:, hh:hh + 1], in1=sc[:, sl(kt_, TT)],
                            op0=ALU.mult, op1=ALU.add), reads=[r.b, wit[j].b, sc.bs[kt_]], writes=[sc.bs[kt_]])
        for j in range(4):
            sc = score[j]
            s_ = sm[j]
            allb = sc.bs[0:nkt]
            P.op("dve", lambda e, sc=sc, s_=s_, S_c=S_c: e.tensor_reduce(out=s_[:, 0:1], in_=sc[:, 0:S_c], axis=AX.X, op=ALU.max),
                 reads=allb, writes=[s_.b])
            P.op("dve", lambda e, sc=sc, s_=s_, S_c=S_c: e.tensor_reduce(out=s_[:, 3:4], in_=sc[:, 0:S_c], axis=AX.X, op=ALU.min),
                 reads=allb, writes=[s_.b])
            P.op("dve", lambda e, s_=s_: e.tensor_tensor(out=s_[:, 2:3], in0=s_[:, 0:1], in1=s_[:, 3:4], op=ALU.subtract),
                 reads=[s_.b], writes=[s_.b])
            for k2 in range(2):
                a = am.next()
                P.dma("sp", "da%d" % a.slot, a[:], io["amask"][i, j, k2, :, :], writes=[a.b])
                kt_ = nkt - 2 + k2
                P.op("dve", lambda e, sc=sc, a=a, kt_=kt_: e.tensor_tensor(
                    out=sc[:, sl(kt_, TT)], in0=sc[:, sl(kt_, TT)], in1=a[:], op=ALU.add),
                    reads=[sc.bs[kt_], a.b], writes=[sc.bs[kt_]])
        for it in range(NIT):
            wk = float(2.0 ** -(it + 1))
            for j in range(4):
                sc, s_ = score[j], sm[j]
                allb = sc.bs[0:nkt]
                P.op("dve", lambda e, s_=s_, wk=wk: e.scalar_tensor_tensor(
                    out=s_[:, 4:5], in0=s_[:, 2:3], scalar=wk, in1=s_[:, 3:4], op0=ALU.mult, op1=ALU.add),
                    reads=[s_.b], writes=[s_.b])
            for j in range(4):
                sc, s_ = score[j], sm[j]
                allb = sc.bs[0:nkt]
                P.op("dve", lambda e, sc=sc, s_=s_, S_c=S_c: e.tensor_scalar(
                    out=junk[:, 0:S_c], in0=sc[:, 0:S_c], scalar1=s_[:, 4:5], scalar2=None,
                    op0=ALU.is_ge, op1=ALU.add, accum_out=s_[:, 5:6]), reads=allb + [s_.b], writes=[s_.b])
            for j in range(4):
                s_ = sm[j]
                P.op("dve", lambda e, s_=s_, wk=wk: e.tensor_scalar(
                    out=s_[:, 6:7], in0=s_[:, 5:6], scalar1=TOPK - 0.5, scalar2=wk, op0=ALU.is_ge, op1=ALU.mult),
                    reads=[s_.b], writes=[s_.b])
            for j in range(4):
                s_ = sm[j]
                P.op("dve", lambda e, s_=s_: e.scalar_tensor_tensor(
                    out=s_[:, 3:4], in0=s_[:, 2:3], scalar=s_[:, 6:7], in1=s_[:, 3:4], op0=ALU.mult, op1=ALU.add),
                    reads=[s_.b], writes=[s_.b])
        for j in range(4):
            sc, s_ = score[j], sm[j]
            m = Mt.next()
            P.op("dve", lambda e, sc=sc, s_=s_, m=m, S_c=S_c: e.tensor_scalar(
                out=m[:, 0:S_c], in0=sc[:, 0:S_c], scalar1=s_[:, 3:4], scalar2=None, op0=ALU.is_ge),
                reads=sc.bs[0:nkt] + [s_.b], writes=[m.b])
            if "dbg_sm" in io:
                P.dma("sp", "dbg", io["dbg_sm"][i, j, :, :], s_[:], reads=[s_.b])
                P.dma("sp", "dbg", io["dbg_M"][i, j, :, 0:S_c], m[:, 0:S_c], reads=[m.b])
                P.dma("sp", "dbg", io["dbg_sc"][i, j, :, 0:S_c], sc[:, 0:S_c], reads=sc.bs[0:nkt])
            for g4 in range(nkb // 4):
                bank = misc.next()
                bb = bank[:].bitcast(BF16)
                for b4 in range(4):
                    P.op("pe", lambda e, bb=bb, m=m, g4=g4, b4=b4: e.transpose(
                        bb[:, sl(b4, 128)], m[:, sl(g4 * 4 + b4, 128)], ident[:]),
                        reads=[m.b, ident.b], writes=[bank.b])
                P.op("act", lambda e, bb=bb, g4=g4, j=j: e.activation(
                    out=MT[:, g4 * 4:(g4 + 1) * 4, sl(j, 128)],
                    in_=bb[:, 0:512].rearrange("p (a b) -> p a b", a=4), func=AF.Copy),
                    reads=[bank.b], writes=[MT.b])
        for g in range(4):
            ktg = KTg.next()
            vg = Vg.next()
            for gb in range(nkt):
                P.dma("sp", "dK%d" % ktg.slot, ktg[:, sl(gb, TT)], ks["kd"](g, gb), reads=[P.dbuf("x_kd")],
                      writes=[ktg.b])
                P.dma("sp", "dK%d" % ktg.slot, vg[:, gb * 4:(gb + 1) * 4, :],
                      ks["vd"](g, gb).rearrange("(jb p) c -> p jb c", p=128), reads=[P.dbuf("x_vd")], writes=[vg.b])
            for hq in range(4):
                h = 4 * g + hq
                q = qt.next()
                P.dma("sp", "dQ%d" % q.slot, q[:], io["qd"][h, :, sl(i, TT)], reads=[P.dbuf("qd")], writes=[q.b])

                def lhs_fn(jb, ST, ktg=ktg, q=q):
                    P.op("pe", lambda e: e.matmul(ST[:], lhsT=ktg[:, sl(jb, 128)], rhs=q[:], start=True, stop=True),
                         reads=[ktg.b, q.b], writes=[ST.b])

                def mask_fn(jb):
                    return MT, MT[:, jb, :]

                def vfn(jb, vg=vg):
                    return vg, vg[:, jb, :]

                def fin(OT, LT, h=h, i=i):
                    r = rl.next()
                    P.op("dve", lambda e: e.reciprocal(out=r[:], in_=LT[:]), reads=[LT.b], writes=[r.b])
                    o = ob.next()
                    P.op("dve", lambda e: e.tensor_tensor(out=o[:], in0=OT[:], in1=r[:], op=ALU.mult),
                         reads=[OT.b, r.b], writes=[o.b])
                    P.dma(STQ, "do%d" % o.slot, io["oT_dsa"][sl(h, 128), sl(i, TT)], o[:], reads=[o.b],
                          writes=[P.dbuf("oT_dsa", h, i)])

                attn_core(cx, nkb, lhs_fn, pt, st_ring, acc_ring, mask_fn, vfn, ones, scale, fin)


def phase_b(cx, io):
    ks = {
        "ckvn": lambda gb: io["ckvn_g"][:, sl(gb, TT)],
        "krope": lambda gb: io["krope_g"][:, sl(gb, TT)],
        "ki": lambda gb: io["ki_g"][:, sl(gb, TT)],
        "kd": lambda g, gb: io["kd_g"][g, :, sl(gb, TT)],
        "vd": lambda g, gb: io["vd_g"][sl(gb, TT), sl(g, 128)],
    }
    phase_b_mla(cx, io, ks)
    cx.P.barrier()
    phase_b_dsa(cx, io, ks)


B_IN = {
    "qm": ([16, 192, NTOK], BF16), "qd": ([16, 128, NTOK], BF16), "qi": ([8, 128, NTOK], BF16),
    "wi": ([NTOK, 16], F32), "ckvn_g": ([512, SEQ], BF16), "krope_g": ([64, SEQ], BF16),
    "kd_g": ([4, 128, SEQ], BF16), "vd_g": ([SEQ, 512], BF16), "ki_g": ([64, SEQ], BF16),
    "w_ukv": ([512, 4096], F32), "cmask": ([4, 8, 128, 512], BF16), "amask": ([4, 4, 2, 128, 512], BF16),
    "ident": ([128, 128], BF16),
}
B_OUT = {"oT_mla": ([D, NTOK], BF16), "oT_dsa": ([D, NTOK], BF16)}


def attn_masks(half):
    cm = np.zeros((4, 8, 128, 512), np.float32)
    amk = np.zeros((4, 4, 2, 128, 512), np.float32)
    for i in range(4):
        gb = BLOCKS[half][i]
        qpos = gb * TT + np.arange(TT)
        for b in range(8):
            kpos = (KMAX[i] - 2) * TT + b * 128 + np.arange(128)
            cm[i, b] = (kpos[:, None] <= qpos[None, :]).astype(np.float32)
        for j in range(4):
            qp = gb * TT + j * 128 + np.arange(128)
            for k2 in range(2):
                kp = (KMAX[i] - 2 + k2) * TT + np.arange(TT)
                amk[i, j, k2] = np.where(kp[None, :] <= qp[:, None], 0.0, -1e30)
    return cm.astype(NPBF), amk.astype(NPBF)


def phase_c(cx, io, last):
    nc, P, ps = cx.nc, cx.P, cx.ps
    cx.arena.reset()
    xs = [T(cx, "xs%d" % k, [128, 512], F32) for k in range(KC)]
    Q = [T(cx, "Q%d" % k, [128, 16, 512], BF16) for k in range(4)]
    cvb, hc, oml, ods = Q
    mu = T(cx, "mu", [128, 16, 512], BF16)
    wst = Ring(cx, "wstC", 2, [128, KC, 256], F32)
    wb = Ring(cx, "wbC", 2, [128, KC, 256], BF16)
    yb = Ring(cx, "ybC", 3, [128, 544], F32)
    acc = Ring(cx, "accC", 4, [128, 512], F32)
    sq = Ring(cx, "sqC", 2, [128, 512], BF16)
    cw = T(cx, "cw", [128, KC, 31], F32)
    vecs = {k: T(cx, "v_" + k, [128, KC], F32) for k in ("conv_b", "ln_g", "ln_b", "mlp_norm", "final_norm")}
    ones = T(cx, "onesC", [128, 128], BF16)
    mean = T(cx, "meanC", [128, 512], F32)
    rstd = T(cx, "rstdC", [128, 512], F32)
    sm = T(cx, "smC", [128, 512], F32)
    tmp = Ring(cx, "tmpC", 4, [128, 512], F32)
    gt = Ring(cx, "gtC", 6, [128, 512], BF16)

    P.op("pool", lambda e: e.memset(ones[:], 1.0), writes=[ones.b])
    P.dma("sp", "cc0", cw[:], io["conv_w"][:, :, :], writes=[cw.b])
    for k, t in vecs.items():
        P.dma("sp", "cc1", t[:], io[k][:, :], writes=[t.b])

    def load_w(src_view3, rows, cols, shape3):
        ws = wst.next()
        a, b = shape3
        wsv = ws[:].rearrange("p a b -> p (a b)").rearrange("p (a b) -> p a b", a=a)
        P.dma("sp", "cw%d" % ws.slot, wsv, src_view3[:, rows, cols], writes=[ws.b])
        w = wb.next()
        wv_ = w[:].rearrange("p a b -> p (a b)").rearrange("p (a b) -> p a b", a=a)
        P.op("pool", lambda e: e.tensor_copy(out=wv_, in_=wsv), reads=[ws.b], writes=[w.b])
        return w, wv_

    def proj(wt, wv_, m0, src_t, src_fn, kcn):
        bank = ps.next()
        for kc in range(kcn):
            st_, sap = src_fn(kc)
            P.op("pe", lambda e, kc=kc, sap=sap: e.matmul(bank[:], lhsT=wv_[:, kc, m0:m0 + 128], rhs=sap,
                                                          start=(kc == 0), stop=(kc == kcn - 1)),
                 reads=[wt.b, st_.b], writes=[bank.b])
        return bank

    def rstd_from(ssbank, n_feat):
        P.op("act", lambda e: e.activation(out=sm[:], in_=ssbank[:], func=AF.Sqrt, scale=1.0 / n_feat, bias=EPS),
             reads=[ssbank.b], writes=[sm.b])
        P.op("dve", lambda e: e.reciprocal(out=rstd[:], in_=sm[:]), reads=[sm.b], writes=[rstd.b])

    def sumsq_x():
        SS = ps.next()
        for oc in range(KC):
            s_ = sq.next()
            P.op("act", lambda e, s_=s_, oc=oc: e.activation(out=s_[:], in_=xs[oc][:], func=AF.Square),
                 reads=[xs[oc].b], writes=[s_.b])
            P.op("pe", lambda e, s_=s_, oc=oc: e.matmul(SS[:], lhsT=ones[:], rhs=s_[:], start=(oc == 0),
                                                        stop=(oc == KC - 1)), reads=[ones.b, s_.b], writes=[SS.b])
        rstd_from(SS, D)

    wviews = {k: io[k].rearrange("(kc p) c -> p kc c", p=128) for k in
              ("w_o_mla", "w_o_dsa", "w_o_conv", "w_out", "w_up", "w_down")}
    xv = io["xT"].rearrange("(kc p) n -> p kc n", p=128)
    inv_d = 1.0 / D

    for tt in range(NT):
        tsl = sl(tt, TT)
        for oc in range(KC):
            P.dma("sp", "cx%d" % (oc % 4), xs[oc][:], io["xT"][sl(oc, 128), tsl], reads=[P.dbuf("xT_in")],
                  writes=[xs[oc].b])
        S1 = ps.next()
        S2 = ps.next()
        for pr in range(8):
            ybs, accs = [], []
            for c2 in range(2):
                cc = 2 * pr + c2
                y = yb.next()
                P.dma("sp", "cy%d" % y.slot, y[:, 0:32], io["halo"][sl(cc, 128), tt, :], reads=[P.dbuf("halo")],
                      writes=[y.b])
                P.dma("sp", "cy%d" % y.slot, y[:, 32:544], io["glu"][sl(cc, 128), tsl], reads=[P.dbuf("glu")],
                      writes=[y.b])
                a = acc.next()
                ybs.append(y)
                accs.append(a)
                P.op("dve", lambda e, y=y, a=a, cc=cc: e.tensor_scalar(
                    out=a[:], in0=y[:, 2:514], scalar1=cw[:, cc, 0:1], scalar2=vecs["conv_b"][:, cc:cc + 1],
                    op0=ALU.mult, op1=ALU.add), reads=[y.b, cw.b, vecs["conv_b"].b], writes=[a.b])
            for k in range(1, 31):
                for c2 in range(2):
                    cc = 2 * pr + c2
                    y, a = ybs[c2], accs[c2]
                    P.op("dve", lambda e, y=y, a=a, cc=cc, k=k: e.scalar_tensor_tensor(
                        out=a[:], in0=y[:, 2 + k:514 + k], scalar=cw[:, cc, k:k + 1], in1=a[:],
                        op0=ALU.mult, op1=ALU.add), reads=[y.b, cw.b, a.b], writes=[a.b])
            for c2 in range(2):
                cc = 2 * pr + c2
                a = accs[c2]
                P.op("act", lambda e, a=a, cc=cc: e.activation(out=cvb[:, cc, :], in_=a[:], func=AF.Copy),
                     reads=[a.b], writes=[cvb.b])
                s_ = sq.next()
                P.op("act", lambda e, a=a, s_=s_: e.activation(out=s_[:], in_=a[:], func=AF.Square),
                     reads=[a.b], writes=[s_.b])
                P.op("pe", lambda e, cc=cc: e.matmul(S1[:], lhsT=ones[:], rhs=cvb[:, cc, :], start=(cc == 0),
                                                     stop=(cc == KC - 1)), reads=[ones.b, cvb.b], writes=[S1.b])
                P.op("pe", lambda e, cc=cc, s_=s_: e.matmul(S2[:], lhsT=ones[:], rhs=s_[:], start=(cc == 0),
                                                            stop=(cc == KC - 1)), reads=[ones.b, s_.b], writes=[S2.b])
        P.op("act", lambda e: e.activation(out=mean[:], in_=S1[:], func=AF.Copy, scale=inv_d),
             reads=[S1.b], writes=[mean.b])
        msq = tmp.next()
        P.op("pool", lambda e, msq=msq: e.tensor_tensor(out=msq[:], in0=mean[:], in1=mean[:], op=ALU.mult),
             reads=[mean.b], writes=[msq.b])
        var = tmp.next()
        P.op("dve", lambda e, msq=msq, var=var: e.scalar_tensor_tensor(
            out=var[:], in0=S2[:], scalar=inv_d, in1=msq[:], op0=ALU.mult, op1=ALU.subtract),
            reads=[S2.b, msq.b], writes=[var.b])
        P.op("act", lambda e, var=var: e.activation(out=sm[:], in_=var[:], func=AF.Sqrt, scale=1.0, bias=EPS),
             reads=[var.b], writes=[sm.b])
        P.op("dve", lambda e: e.reciprocal(out=rstd[:], in_=sm[:]), reads=[sm.b], writes=[rstd.b])
        for cc in range(KC):
            t1 = tmp.next()
            P.op("pool", lambda e, t1=t1, cc=cc: e.tensor_tensor(out=t1[:], in0=cvb[:, cc, :], in1=mean[:],
                                                                 op=ALU.subtract),
                 reads=[cvb.b, mean.b], writes=[t1.b])
            P.op("pool", lambda e, t1=t1: e.tensor_tensor(out=t1[:], in0=t1[:], in1=rstd[:], op=ALU.mult),
                 reads=[t1.b, rstd.b], writes=[t1.b])
            P.op("act", lambda e, t1=t1, cc=cc: e.activation(
                out=hc[:, cc, :], in_=t1[:], func=AF.Silu, scale=vecs["ln_g"][:, cc:cc + 1],
                bias=vecs["ln_b"][:, cc:cc + 1]), reads=[t1.b, vecs["ln_g"].b, vecs["ln_b"].b], writes=[hc.b])
        P.dma("sp", "co0", oml[:], io["oT_mla"].rearrange("(kc p) n -> p kc n", p=128)[:, :, tsl],
              reads=[P.dbuf("oT_mla_in")], writes=[oml.b])
        P.dma("sp", "co1", ods[:], io["oT_dsa"].rearrange("(kc p) n -> p kc n", p=128)[:, :, tsl],
              reads=[P.dbuf("oT_dsa_in")], writes=[ods.b])
        for ocp in range(8):
            banks = {}
            for br, (wk, src) in enumerate((("w_o_mla", oml), ("w_o_dsa", ods), ("w_o_conv", hc))):
                wt, wv_ = load_w(wviews[wk], slice(0, KC), sl(ocp, 256), (KC, 256))
                for o2 in range(2):
                    banks[(br, o2)] = proj(wt, wv_, o2 * 128, src, lambda kc, src=src: (src, src[:, kc, :]), KC)
            for o2 in range(2):
                oc = 2 * ocp + o2
                ts_ = []
                for br in range(3):
                    g_ = gt.next()
                    P.dma("sp", "cg%d" % g_.slot, g_[:], io["gates"][br * D + oc * 128:br * D + (oc + 1) * 128, tsl],
                          reads=[P.dbuf("gates")], writes=[g_.b])
                    t_ = tmp.next()
                    bk = banks[(br, o2)]
                    P.op("dve", lambda e, t_=t_, bk=bk, g_=g_: e.tensor_tensor(out=t_[:], in0=bk[:], in1=g_[:],
                                                                               op=ALU.mult),
                         reads=[bk.b, g_.b], writes=[t_.b])
                    ts_.append(t_)
                P.op("pool", lambda e, a=ts_[0], b=ts_[1]: e.tensor_tensor(out=a[:], in0=a[:], in1=b[:], op=ALU.add),
                     reads=[ts_[0].b, ts_[1].b], writes=[ts_[0].b])
                P.op("pool", lambda e, a=ts_[0], b=ts_[2], oc=oc: e.tensor_tensor(out=mu[:, oc, :], in0=a[:], in1=b[:],
                                                                                 op=ALU.add),
                     reads=[ts_[0].b, ts_[2].b], writes=[mu.b])
        for ocp in range(8):
            wt, wv_ = load_w(wviews["w_out"], slice(0, KC), sl(ocp, 256), (KC, 256))
            for o2 in range(2):
                oc = 2 * ocp + o2
                bk = proj(wt, wv_, o2 * 128, mu, lambda kc: (mu, mu[:, kc, :]), KC)
                P.op("dve", lambda e, bk=bk, oc=oc: e.tensor_tensor(out=xs[oc][:], in0=bk[:], in1=xs[oc][:], op=ALU.add),
                     reads=[bk.b, xs[oc].b], writes=[xs[oc].b])
        sumsq_x()
        for oc in range(KC):
            P.op("dve", lambda e, oc=oc: e.scalar_tensor_tensor(
                out=mu[:, oc, :], in0=xs[oc][:], scalar=vecs["mlp_norm"][:, oc:oc + 1], in1=rstd[:],
                op0=ALU.mult, op1=ALU.mult), reads=[xs[oc].b, vecs["mlp_norm"].b, rstd.b], writes=[mu.b])
        for fcp in range(32):
            wt, wv_ = load_w(wviews["w_up"], slice(0, KC), sl(fcp, 256), (KC, 256))
            for f2 in range(2):
                fc = 2 * fcp + f2
                bk = proj(wt, wv_, f2 * 128, mu, lambda kc: (mu, mu[:, kc, :]), KC)
                r = tmp.next()
                P.op("act", lambda e, bk=bk, r=r: e.activation(out=r[:], in_=bk[:], func=AF.Relu),
                     reads=[bk.b], writes=[r.b])
                qd_ = Q[fc // 16]
                P.op("pool", lambda e, r=r, qd_=qd_, fc=fc: e.tensor_tensor(out=qd_[:, fc % 16, :], in0=r[:], in1=r[:],
                                                                          op=ALU.mult),
                     reads=[r.b], writes=[qd_.b])
        for og in range(4):
            banks = [ps.next() for _ in range(4)]
            for k8 in range(8):
                wt, wv_ = load_w(wviews["w_down"], slice(k8 * 8, (k8 + 1) * 8), sl(og, 512), (8, 512))
                for kk in range(8):
                    kc = k8 * 8 + kk
                    qd_ = Q[kc // 16]
                    for o4 in range(4):
                        bk = banks[o4]
                        P.op("pe", lambda e, bk=bk, kk=kk, o4=o4, kc=kc, qd_=qd_, wv_=wv_: e.matmul(
                            bk[:], lhsT=wv_[:, kk, sl(o4, 128)], rhs=qd_[:, kc % 16, :],
                            start=(kc == 0), stop=(kc == 63)), reads=[wt.b, qd_.b], writes=[bk.b])
            for o4 in range(4):
                oc = og * 4 + o4
                bk = banks[o4]
                P.op("dve", lambda e, bk=bk, oc=oc: e.tensor_tensor(out=xs[oc][:], in0=bk[:], in1=xs[oc][:], op=ALU.add),
                     reads=[bk.b, xs[oc].b], writes=[xs[oc].b])
        if last:
            sumsq_x()
            for oc in range(KC):
                o = tmp.next()
                P.op("dve", lambda e, oc=oc, o=o: e.scalar_tensor_tensor(
                    out=o[:], in0=xs[oc][:], scalar=vecs["final_norm"][:, oc:oc + 1], in1=rstd[:],
                    op0=ALU.mult, op1=ALU.mult), reads=[xs[oc].b, vecs["final_norm"].b, rstd.b], writes=[o.b])
                P.dma(STQ, "cs%d" % o.slot, io["xT_out"][sl(oc, 128), tsl], o[:], reads=[o.b],
                      writes=[P.dbuf("xT_out", oc, tt)])
        else:
            for oc in range(KC):
                P.dma(STQ, "cs%d" % (oc % 4), io["xT_out"][sl(oc, 128), tsl], xs[oc][:], reads=[xs[oc].b],
                      writes=[P.dbuf("xT_out", oc, tt)])


C_IN = {
    "xT": ([D, NTOK], F32), "glu": ([D, NTOK], F32), "halo": ([D, 4, 32], F32), "gates": ([3 * D, NTOK], BF16),
    "oT_mla": ([D, NTOK], BF16), "oT_dsa": ([D, NTOK], BF16),
    "conv_w": ([128, KC, 31], F32), "conv_b": ([128, KC], F32), "ln_g": ([128, KC], F32), "ln_b": ([128, KC], F32),
    "mlp_norm": ([128, KC], F32), "final_norm": ([128, KC], F32),
    "w_o_mla": ([D, D], F32), "w_o_dsa": ([D, D], F32), "w_o_conv": ([D, D], F32), "w_out": ([D, D], F32),
    "w_up": ([D, DFF], F32), "w_down": ([DFF, D], F32),
}
C_OUT = {"xT_out": ([D, NTOK], F32)}


_PROGS = {}


def _prog(name):
    if name not in _PROGS:
        if name == "A":
            _PROGS[name] = build_phase(phase_a, A_IN, A_OUT)
        elif name == "B":
            _PROGS[name] = build_phase(phase_b, B_IN, B_OUT)
        elif name == "C0":
            _PROGS[name] = build_phase(lambda cx, io: phase_c(cx, io, False), C_IN, C_OUT)
        elif name == "C1":
            _PROGS[name] = build_phase(lambda cx, io: phase_c(cx, io, True), C_IN, C_OUT)
    return _PROGS[name]


def _gather_global(resA, name, b, axis):
    parts = [None] * 8
    for half in range(2):
        a = np.asarray(resA[2 * b + half][name])
        for i, gb in enumerate(BLOCKS[half]):
            parts[gb] = np.take(a, np.arange(i * TT, (i + 1) * TT), axis=axis)
    return np.ascontiguousarray(np.concatenate(parts, axis=axis))


def _halo(resA, b):
    full = _gather_global(resA, "glu", b, 1)
    out = []
    for half in range(2):
        h = np.zeros((D, 4, 32), np.float32)
        for i, gb in enumerate(BLOCKS[half]):
            if gb > 0:
                h[:, i, :] = full[:, gb * TT - 32:gb * TT]
        out.append(h)
    return out


def kernel(x, attn_norm, w_in, mla_q_norm, mla_kv_norm, mla_w_uq, mla_w_ukv,
           conv_w_dw, conv_b_dw, conv_ln_g, conv_ln_b,
           w_o_mla, w_o_dsa, w_o_conv, w_out, mlp_norm, w_up, w_down, final_norm):
    f32 = lambda a: np.ascontiguousarray(np.asarray(a, dtype=np.float32))
    x = f32(x)
    cores = list(range(NCORE))
    tok = [core_tokens(c) for c in cores]
    xT = [np.ascontiguousarray(x[b, idx, :].T) for (b, idx) in tok]
    tabs = [rope_tables(idx) for (_, idx) in tok]
    masks = [attn_masks(h) for h in range(2)]
    ident = np.eye(128, dtype=np.float32).astype(NPBF)
    depth = np.asarray(w_in).shape[0]
    for L in range(depth):
        last = L == depth - 1
        in_maps = []
        for c in cores:
            m = {"xT": xT[c], "w_in": f32(w_in[L]), "attn_norm": vec_layout(f32(attn_norm[L])),
                 "q_norm": vec_layout(f32(mla_q_norm[L])), "kv_norm": vec_layout(f32(mla_kv_norm[L])),
                 "w_uq": f32(mla_w_uq[L])}
            m.update(tabs[c])
            in_maps.append(m)
        rA = run_bass_kernel_spmd(_prog("A"), in_maps, core_ids=cores).results
        in_maps = []
        gl = {}
        for b in range(NB):
            gl[b] = {"ckvn_g": _gather_global(rA, "ckvn", b, 1), "krope_g": _gather_global(rA, "krope", b, 1),
                     "kd_g": _gather_global(rA, "kd", b, 2), "vd_g": _gather_global(rA, "vd", b, 0),
                     "ki_g": _gather_global(rA, "ki", b, 1)}
        for c in cores:
            b, half = c // 2, c % 2
            m = {"qm": np.asarray(rA[c]["qm"]), "qd": np.asarray(rA[c]["qd"]), "qi": np.asarray(rA[c]["qi"]),
                 "wi": np.asarray(rA[c]["wi"]), "w_ukv": f32(mla_w_ukv[L]),
                 "cmask": masks[half][0], "amask": masks[half][1], "ident": ident}
            m.update(gl[b])
            in_maps.append(m)
        rB = run_bass_kernel_spmd(_prog("B"), in_maps, core_ids=cores).results
        halos = {b: _halo(rA, b) for b in range(NB)}
        in_maps = []
        for c in cores:
            b, half = c // 2, c % 2
            m = {"xT": xT[c], "glu": np.asarray(rA[c]["glu"]), "halo": halos[b][half],
                 "gates": np.asarray(rA[c]["gates"]), "oT_mla": np.asarray(rB[c]["oT_mla"]),
                 "oT_dsa": np.asarray(rB[c]["oT_dsa"]),
                 "conv_w": np.ascontiguousarray(f32(conv_w_dw[L]).reshape(31, KC, 128).transpose(2, 1, 0)),
                 "conv_b": vec_layout(f32(conv_b_dw[L])), "ln_g": vec_layout(f32(conv_ln_g[L])),
                 "ln_b": vec_layout(f32(conv_ln_b[L])), "mlp_norm": vec_layout(f32(mlp_norm[L])),
                 "final_norm": vec_layout(f32(final_norm)),
                 "w_o_mla": f32(w_o_mla[L]), "w_o_dsa": f32(w_o_dsa[L]), "w_o_conv": f32(w_o_conv[L]),
                 "w_out": f32(w_out[L]), "w_up": f32(w_up[L]), "w_down": f32(w_down[L])}
            in_maps.append(m)
        rC = run_bass_kernel_spmd(_prog("C1" if last else "C0"), in_maps, core_ids=cores).results
        xT = [np.ascontiguousarray(np.asarray(rC[c]["xT_out"], dtype=np.float32)) for c in cores]
    out = np.empty((NB, SEQ, D), np.float32)
    for c in cores:
        b, idx = tok[c]
        out[b, idx, :] = xT[c].T
    return out
```

```python
import contextlib
import numpy as np
import ml_dtypes
import concourse.bass as bass
import concourse.mybir as mybir
from concourse.bass_utils import run_bass_kernel_spmd

F32 = mybir.dt.float32
BF16 = mybir.dt.bfloat16
ALU = mybir.AluOpType
AF = mybir.ActivationFunctionType
AX = mybir.AxisListType
NPBF = ml_dtypes.bfloat16

D = 2048
SEQ = 4096
NB = 4
NCORE = 8
NTOK = 2048
TT = 512
NT = 4
KC = 16
DFF = 8192
EPS = 1e-6
THETA = 500000.0
IN_COLS = 15504
C_CQ, C_CKV, C_KR, C_QD, C_KD, C_VD, C_QI, C_KI, C_WI, C_GA, C_GG, C_GT = (
    0, 512, 1024, 1088, 3136, 3648, 4160, 5184, 5248, 5264, 7312, 9360)
BLOCKS = ((0, 3, 4, 7), (1, 2, 5, 6))
STQ = "act"

ENGS = ("pe", "act", "dve", "pool", "sp")


class Buf:
    __slots__ = ("name", "w", "r")

    def __init__(self, name):
        self.name = name
        self.w = None
        self.r = {}


class Op:
    __slots__ = ("eng", "fn", "deps", "dwaits", "sig", "dsem", "dval", "idx", "dinc")


class Prog:
    def __init__(self, nc):
        self.nc = nc
        self.ops = []
        self.dma_streams = {}
        self.dbufs = {}
        self.nobar = set()

    def dbuf(self, *key):
        b = self.dbufs.get(key)
        if b is None:
            b = Buf(str(key))
            self.dbufs[key] = b
        return b

    def _record(self, eng, fn, reads, writes, dsem=None, dinc=16):
        op = Op()
        op.eng = eng
        op.fn = fn
        op.sig = dsem is not None
        op.dsem = dsem
        op.dval = 0
        op.dinc = dinc
        op.idx = len(self.ops)
        deps = set()
        for b in reads:
            if b.w is not None:
                deps.add(b.w)
        for b in writes:
            if b.w is not None:
                deps.add(b.w)
            for r in b.r.values():
                deps.add(r)
        cdeps = set()
        dwaits = {}
        for d in deps:
            p = self.ops[d]
            if p.dsem is not None:
                if p.dinc == 1:
                    dwaits[p.dsem] = max(dwaits.get(p.dsem, 0), p.dval)
                else:
                    dwaits[p.dsem] = self.dma_streams[p.dsem][0]
            elif not (eng == "pe" and dsem is None and p.eng == "pe"):
                cdeps.add(d)
        op.deps = cdeps
        op.dwaits = dwaits
        if dsem is not None:
            c = self.dma_streams.setdefault(dsem, [0])
            c[0] += dinc
            op.dval = c[0]
        self.ops.append(op)
        rkey = eng if dsem is None else "dma_" + dsem
        for b in reads:
            b.r[rkey] = op.idx
        for b in writes:
            b.w = op.idx
            b.r = {}
        return op

    def op(self, eng, fn, reads=(), writes=()):
        return self._record(eng, fn, list(reads), list(writes))

    def dma(self, queue, stream, out, in_, reads=(), writes=()):
        def fn(e, out=out, in_=in_):
            return e.dma_start(out=out, in_=in_)
        return self._record(queue, fn, list(reads), list(writes), dsem=stream)

    def custom_dma(self, queue, stream, fn, reads=(), writes=(), inc=16):
        return self._record(queue, fn, list(reads), list(writes), dsem=stream, dinc=inc)

    def barrier(self):
        last = {}
        for op in self.ops:
            if op.dsem is None:
                last[op.eng] = op.idx
        for e in ENGS:
            op = self._record(e, lambda eng: eng.nop(), [], [])
            for e2, i in last.items():
                if e2 != e:
                    op.deps.add(i)
            for k, c in self.dma_streams.items():
                if k not in self.nobar:
                    op.dwaits[k] = c[0]

    def emit(self, final_wait_eng="sp"):
        nc = self.nc
        ops = self.ops
        for op in ops:
            for d in op.deps:
                ops[d].sig = True
        cnt = {e: 0 for e in ENGS}
        for op in ops:
            if op.dsem is None and op.sig:
                cnt[op.eng] += 1
                op.dval = cnt[op.eng]
        with contextlib.ExitStack() as st:
            esem = {e: st.enter_context(nc.semaphore("s_" + e)) for e in ENGS}
            dsem = {k: st.enter_context(nc.semaphore("d_" + k)) for k in self.dma_streams}
            block = st.enter_context(nc.Block())
            per_eng = {e: [] for e in ENGS}
            for op in ops:
                per_eng[op.eng].append(op)

            def run(eng_name, eobj):
                waited = {}
                for op in per_eng[eng_name]:
                    need = {}
                    for d in op.deps:
                        p = ops[d]
                        key = ("e", p.eng)
                        if p.dval > need.get(key, 0):
                            need[key] = p.dval
                    for k, v in op.dwaits.items():
                        need[("d", k)] = v
                    for key, v in need.items():
                        if waited.get(key, 0) >= v:
                            continue
                        waited[key] = v
                        s = dsem[key[1]] if key[0] == "d" else esem[key[1]]
                        eobj.wait_ge(s, v)
                    ins = op.fn(eobj)
                    if op.dsem is not None:
                        ins.then_inc(dsem[op.dsem], op.dinc)
                    elif op.sig:
                        ins.then_inc(esem[op.eng], 1)
                if eng_name == final_wait_eng:
                    for k, c in self.dma_streams.items():
                        eobj.wait_ge(dsem[k], c[0])
                    for e in ENGS:
                        if cnt[e] > 0 and e != eng_name:
                            eobj.wait_ge(esem[e], cnt[e])

            @block.tensor
            def _(e):
                run("pe", e)

            @block.scalar
            def _(e):
                run("act", e)

            @block.vector
            def _(e):
                run("dve", e)

            @block.gpsimd
            def _(e):
                run("pool", e)

            @block.sync
            def _(e):
                run("sp", e)


DSIZE = {F32: 4, BF16: 2}


class Arena:
    def __init__(self, nc, nbytes=204 * 1024):
        self.h = nc.alloc_sbuf_tensor("arena", [128, nbytes], mybir.dt.uint8)
        self.cap = nbytes
        self.off = 0

    def reset(self):
        self.off = 0

    def alloc(self, shape, dtype):
        n = 1
        for d in shape[1:]:
            n *= d
        nb = (n * DSIZE[dtype] + 31) // 32 * 32
        assert self.off + nb <= self.cap, "arena overflow %d + %d" % (self.off, nb)
        ap = self.h[0:shape[0], self.off:self.off + n * DSIZE[dtype]].bitcast(dtype)
        self.off += nb
        if len(shape) == 3:
            ap = ap.rearrange("p (a b) -> p a b", a=shape[1])
        elif len(shape) == 4:
            ap = ap.rearrange("p (a b c) -> p a b c", a=shape[1], b=shape[2])
        return ap


class Ctx:
    def __init__(self, nc):
        self.nc = nc
        self.P = Prog(nc)
        self.arena = Arena(nc)
        self.ps = Ring(self, "ps", 8, [128, 512], F32, psum=True)


class T:
    def __init__(self, cx, name, shape, dtype, psum=False, buf=None):
        if psum:
            self.ap = cx.nc.alloc_psum_tensor(name, shape, dtype)[:]
        else:
            self.ap = cx.arena.alloc(shape, dtype)
        self.b = buf if buf is not None else Buf(name)
        self.name = name
        self.slot = 0

    def __getitem__(self, k):
        return self.ap[k]


class Ring:
    def __init__(self, cx, name, n, shape, dtype, psum=False):
        self.ts = [T(cx, "%s%d" % (name, i), shape, dtype, psum=psum) for i in range(n)]
        self.i = 0
        self.name = name

    def next(self):
        t = self.ts[self.i % len(self.ts)]
        t.slot = self.i % len(self.ts)
        self.i += 1
        return t


def sl(i, n):
    return slice(i * n, (i + 1) * n)


def wd(io, key):
    d = io.get("_wdeps")
    return [d[key]] if d and key in d else []


def phase_a(cx, io):
    nc, P, ps = cx.nc, cx.P, cx.ps
    uT = T(cx, "uT", [128, KC, NTOK], BF16)
    cqn = T(cx, "cqn", [128, 4, NTOK], BF16)
    ones = T(cx, "onesA", [128, 128], BF16)
    gA = T(cx, "gA", [128, KC], F32)
    gq = T(cx, "gq", [128, 4], F32)
    gkv = T(cx, "gkv", [128, 4], F32)
    wst = Ring(cx, "wst", 2, [128, KC, 256], F32)
    wb = Ring(cx, "wb", 2, [128, KC, 256], BF16)
    wsw = Ring(cx, "wsw", 2, [128, KC, 256], BF16)
    tabs = Ring(cx, "tabs", 4, [128, 512], F32)
    tmp = Ring(cx, "tmpA", 6, [128, 512], F32)
    outs = Ring(cx, "outA", 6, [128, 512], F32)
    rstd = T(cx, "rstdA", [128, 512], F32)
    sm = T(cx, "smA", [128, 512], F32)

    P.op("pool", lambda e: e.memset(ones[:], 1.0), writes=[ones.b])
    P.dma("sp", "c0", gA[:], io["attn_norm"][:, :], writes=[gA.b])
    P.dma("sp", "c1", gq[:], io["q_norm"][:, :], writes=[gq.b])
    P.dma("sp", "c2", gkv[:], io["kv_norm"][:, :], writes=[gkv.b])

    def rstd_from(ssbank, n_feat):
        P.op("act", lambda e: e.activation(out=sm[:], in_=ssbank[:], func=AF.Sqrt,
                                           scale=1.0 / n_feat, bias=EPS),
             reads=[ssbank.b], writes=[sm.b])
        P.op("dve", lambda e: e.reciprocal(out=rstd[:], in_=sm[:]), reads=[sm.b], writes=[rstd.b])

    xv = io["xT"].rearrange("(kc p) n -> p kc n", p=128)
    for tt in range(NT):
        xh, qh = [], []
        for hf in range(2):
            ws = wst.next()
            xs = ws[:].rearrange("p a b -> p (a b)").rearrange("p (kc n) -> p kc n", kc=8)
            P.dma("sp", "w%d" % ws.slot, xs, xv[:, sl(hf, 8), sl(tt, TT)], writes=[ws.b])
            wq = wb.next()
            qs = wq[:].rearrange("p a b -> p (a b)").rearrange("p (kc n) -> p kc n", kc=8)
            P.op("act", lambda e, qs=qs, xs=xs: e.activation(out=qs, in_=xs, func=AF.Square),
                 reads=[ws.b], writes=[wq.b])
            xh.append((ws, xs))
            qh.append((wq, qs))
        ss = ps.next()
        for kc in range(KC):
            wq, qs = qh[kc // 8]
            P.op("pe", lambda e, kc=kc, ss=ss, qs=qs: e.matmul(ss[:], lhsT=ones[:], rhs=qs[:, kc % 8, :],
                                                               start=(kc == 0), stop=(kc == KC - 1)),
                 reads=[ones.b, wq.b], writes=[ss.b])
        rstd_from(ss, D)
        for kc in range(KC):
            eng = "dve"
            ws, xs = xh[kc // 8]
            P.op(eng, lambda e, kc=kc, tt=tt, xs=xs: e.scalar_tensor_tensor(
                out=uT[:, kc, sl(tt, TT)], in0=xs[:, kc % 8, :], scalar=gA[:, kc:kc + 1],
                in1=rstd[:], op0=ALU.mult, op1=ALU.mult),
                reads=[ws.b, gA.b, rstd.b], writes=[uT.b])

    def load_w(src_view, segs, kcn=KC, rot=None):
        ws = wst.next()
        o = 0
        rd = wd(io, "w_uq" if kcn == 4 else "w_in")
        for (c0, n) in segs:
            P.dma("sp", "w%d" % ws.slot, ws[:, 0:kcn, o:o + n], src_view[:, :, c0:c0 + n], reads=rd, writes=[ws.b])
            o += n
        w = wb.next()
        if wb.i % 2 == 0:
            P.op("act", lambda e: e.activation(out=w[:, 0:kcn, 0:o], in_=ws[:, 0:kcn, 0:o], func=AF.Copy),
                 reads=[ws.b], writes=[w.b])
        else:
            P.op("pool", lambda e: e.tensor_copy(out=w[:, 0:kcn, 0:o], in_=ws[:, 0:kcn, 0:o]),
                 reads=[ws.b], writes=[w.b])
        sw = None
        if rot is not None:
            sw = wsw.next()
            P.op("pool", lambda e: e.memset(sw[:, 0:kcn, 0:o], 0.0), writes=[sw.b])
            for (r0, hf) in rot:
                P.op("pool", lambda e, r0=r0, hf=hf: e.tensor_scalar(
                    out=sw[:, 0:kcn, r0:r0 + hf], in0=ws[:, 0:kcn, r0 + hf:r0 + 2 * hf],
                    scalar1=-1.0, scalar2=None, op0=ALU.mult), reads=[ws.b], writes=[sw.b])
                P.op("pool", lambda e, r0=r0, hf=hf: e.tensor_copy(
                    out=sw[:, 0:kcn, r0 + hf:r0 + 2 * hf], in_=ws[:, 0:kcn, r0:r0 + hf]),
                    reads=[ws.b], writes=[sw.b])
        return w, sw

    def proj(w, m0, M, src, kcn, tt):
        bank = ps.next()
        for kc in range(kcn):
            P.op("pe", lambda e, kc=kc, bank=bank: e.matmul(
                bank[0:M, :], lhsT=w[:, kc, m0:m0 + M], rhs=src[:, kc, sl(tt, TT)],
                start=(kc == 0), stop=(kc == kcn - 1)), reads=[w.b, src.b], writes=[bank.b])
        return bank

    def store(dst_ap, t, ap, key):
        P.dma(STQ, "o%s%d" % (t.name[:4], t.slot), dst_ap, ap, reads=[t.b], writes=[P.dbuf(*key)])

    def load_tab(tab_ap, rows, tt):
        t = tabs.next()
        P.dma("sp", "tb%d" % t.slot, t[0:rows, :], tab_ap[:, sl(tt, TT)], writes=[t.b])
        return t

    def rope_epilogue(pre, swp, M, tabC, tabS, dst_ap, key):
        t1 = tmp.next()
        P.op("act", lambda e: e.activation(out=t1[0:M, :], in_=swp[0:M, :], func=AF.Copy),
             reads=[swp.b], writes=[t1.b])
        t2 = tmp.next()
        P.op("pool", lambda e: e.tensor_tensor(out=t2[0:M, :], in0=t1[0:M, :], in1=tabS[0:M, :], op=ALU.mult),
             reads=[t1.b, tabS.b], writes=[t2.b])
        t3 = tmp.next()
        P.op("dve", lambda e: e.tensor_tensor(out=t3[0:M, :], in0=pre[0:M, :], in1=tabC[0:M, :], op=ALU.mult),
             reads=[pre.b, tabC.b], writes=[t3.b])
        o = outs.next()
        ob = o[:].bitcast(BF16)
        P.op("pool", lambda e: e.tensor_tensor(out=ob[0:M, 0:512], in0=t3[0:M, :], in1=t2[0:M, :], op=ALU.add),
             reads=[t3.b, t2.b], writes=[o.b])
        store(dst_ap, o, ob[0:M, 0:512], key)

    def copy_epilogue(pre, M, dst_ap, key, func=AF.Copy):
        o = outs.next()
        ob = o[:].bitcast(BF16)
        P.op("act", lambda e: e.activation(out=ob[0:M, 0:512], in_=pre[0:M, :], func=func),
             reads=[pre.b], writes=[o.b])
        store(dst_ap, o, ob[0:M, 0:512], key)

    wv = io["w_in"].rearrange("(kc p) c -> p kc c", p=128)

    def latent(c0, g, dst_sb, dst_dram, nm):
        wts = [load_w(wv, [(c0 + 256 * i, 256)])[0] for i in range(2)]
        for tt in range(NT):
            banks = []
            ssb = None
            for j in range(4):
                bk = proj(wts[j // 2], (j % 2) * 128, 128, uT, KC, tt)
                banks.append(bk)
                sq = tmp.next()
                sqb = sq[:].bitcast(BF16)
                P.op("act", lambda e, bk=bk, sqb=sqb: e.activation(out=sqb[:, 0:512], in_=bk[:], func=AF.Square),
                     reads=[bk.b], writes=[sq.b])
                if ssb is None:
                    ssb = ps.next()
                P.op("pe", lambda e, j=j, sqb=sqb, ssb=ssb: e.matmul(ssb[:], lhsT=ones[:], rhs=sqb[:, 0:512],
                                                                      start=(j == 0), stop=(j == 3)),
                     reads=[ones.b, sq.b], writes=[ssb.b])
            rstd_from(ssb, 512)
            for j in range(4):
                bk = banks[j]
                if dst_sb is not None:
                    P.op("dve", lambda e, j=j, bk=bk, tt=tt: e.scalar_tensor_tensor(
                        out=dst_sb[:, j, sl(tt, TT)], in0=bk[:], scalar=g[:, j:j + 1], in1=rstd[:],
                        op0=ALU.mult, op1=ALU.mult), reads=[bk.b, g.b, rstd.b], writes=[dst_sb.b])
                else:
                    o = outs.next()
                    ob = o[:].bitcast(BF16)
                    P.op("dve", lambda e, j=j, bk=bk, ob=ob: e.scalar_tensor_tensor(
                        out=ob[:, 0:512], in0=bk[:], scalar=g[:, j:j + 1], in1=rstd[:],
                        op0=ALU.mult, op1=ALU.mult), reads=[bk.b, g.b, rstd.b], writes=[o.b])
                    store(dst_dram[sl(j, 128), sl(tt, TT)], o, ob[:, 0:512], (nm, j, tt))

    latent(C_CQ, gq, cqn, None, "cq")
    latent(C_CKV, gkv, None, io["ckvn"], "ckvn")

    wuq = io["w_uq"].rearrange("(kc p) c -> p kc c", p=128)
    for h in range(16):
        w, sw = load_w(wuq, [(h * 192, 192)], kcn=4, rot=[(128, 32)])
        for tt in range(NT):
            bk = proj(w, 0, 128, cqn, 4, tt)
            copy_epilogue(bk, 128, io["qm"][h, 0:128, sl(tt, TT)], ("qm", h, 0, tt))
            pre = proj(w, 128, 64, cqn, 4, tt)
            swp = proj(sw, 128, 64, cqn, 4, tt)
            tc_ = load_tab(io["tabM_C"], 64, tt)
            ts_ = load_tab(io["tabM_S"], 64, tt)
            rope_epilogue(pre, swp, 64, tc_, ts_, io["qm"][h, 128:192, sl(tt, TT)], ("qm", h, 1, tt))

    w, sw = load_w(wv, [(C_KR, 64)], rot=[(0, 32)])
    for tt in range(NT):
        pre = proj(w, 0, 64, uT, KC, tt)
        swp = proj(sw, 0, 64, uT, KC, tt)
        rope_epilogue(pre, swp, 64, load_tab(io["tabM_C"], 64, tt), load_tab(io["tabM_S"], 64, tt),
                      io["krope"][:, sl(tt, TT)], ("krope", tt))

    for (c0, nheads, dst, nm) in ((C_QD, 16, io["qd"], "qd"), (C_KD, 4, io["kd"], "kd")):
        for i in range(nheads // 2):
            w, sw = load_w(wv, [(c0 + 256 * i, 256)], rot=[(0, 16), (128, 16)])
            for tt in range(NT):
                tc_ = load_tab(io["tabA_C"], 128, tt)
                ts_ = load_tab(io["tabA_S"], 128, tt)
                for j in range(2):
                    pre = proj(w, j * 128, 128, uT, KC, tt)
                    swp = proj(sw, j * 128, 128, uT, KC, tt)
                    rope_epilogue(pre, swp, 128, tc_, ts_, dst[2 * i + j, :, sl(tt, TT)], (nm, 2 * i + j, tt))

    for i in range(4):
        w, sw = load_w(wv, [(C_QI + 256 * i, 256)], rot=[(0, 8), (64, 8), (128, 8), (192, 8)])
        for tt in range(NT):
            tc_ = load_tab(io["tabI_C"], 128, tt)
            ts_ = load_tab(io["tabI_S"], 128, tt)
            for j in range(2):
                pre = proj(w, j * 128, 128, uT, KC, tt)
                swp = proj(sw, j * 128, 128, uT, KC, tt)
                rope_epilogue(pre, swp, 128, tc_, ts_, io["qi"][2 * i + j, :, sl(tt, TT)], ("qi", 2 * i + j, tt))
    w, sw = load_w(wv, [(C_KI, 64)], rot=[(0, 8)])
    for tt in range(NT):
        pre = proj(w, 0, 64, uT, KC, tt)
        swp = proj(sw, 0, 64, uT, KC, tt)
        rope_epilogue(pre, swp, 64, load_tab(io["tabI_C"], 128, tt), load_tab(io["tabI_S"], 128, tt),
                      io["ki"][:, sl(tt, TT)], ("ki", tt))

    for i in range(2):
        w, _ = load_w(wv, [(C_VD + 256 * i, 256)])
        for tb in range(NTOK // 128):
            bank = ps.next()
            for kc in range(KC):
                P.op("pe", lambda e, kc=kc, bank=bank, tb=tb, w=w: e.matmul(
                    bank[:, 0:256], lhsT=uT[:, kc, sl(tb, 128)], rhs=w[:, kc, 0:256],
                    start=(kc == 0), stop=(kc == KC - 1)), reads=[w.b, uT.b], writes=[bank.b])
            o = outs.next()
            ob = o[:].bitcast(BF16)
            P.op("act", lambda e, bank=bank, ob=ob: e.activation(out=ob[:, 0:256], in_=bank[:, 0:256], func=AF.Copy),
                 reads=[bank.b], writes=[o.b])
            store(io["vd"][sl(tb, 128), sl(i, 256)], o, ob[:, 0:256], ("vd", tb, i))
    w, _ = load_w(wv, [(C_WI, 16)])
    for tb in range(NTOK // 128):
        bank = ps.next()
        for kc in range(KC):
            P.op("pe", lambda e, kc=kc, bank=bank, tb=tb, w=w: e.matmul(
                bank[:, 0:16], lhsT=uT[:, kc, sl(tb, 128)], rhs=w[:, kc, 0:16],
                start=(kc == 0), stop=(kc == KC - 1)), reads=[w.b, uT.b], writes=[bank.b])
        o = outs.next()
        P.op("act", lambda e, bank=bank, o=o: e.activation(out=o[:, 0:16], in_=bank[:, 0:16], func=AF.Copy,
                                                           scale=float(1024.0 ** -0.5)),
             reads=[bank.b], writes=[o.b])
        store(io["wi"][sl(tb, 128), :], o, o[:, 0:16], ("wi", tb))

    for j in range(16):
        w, _ = load_w(wv, [(C_GA + 128 * j, 128), (C_GG + 128 * j, 128)])
        for tt in range(NT):
            pa = proj(w, 0, 128, uT, KC, tt)
            pg = proj(w, 128, 128, uT, KC, tt)
            sg = tmp.next()
            P.op("act", lambda e, pg=pg, sg=sg: e.activation(out=sg[:], in_=pg[:], func=AF.Sigmoid),
                 reads=[pg.b], writes=[sg.b])
            o = outs.next()
            P.op("dve", lambda e, pa=pa, sg=sg, o=o: e.tensor_tensor(out=o[:], in0=pa[:], in1=sg[:], op=ALU.mult),
                 reads=[pa.b, sg.b], writes=[o.b])
            store(io["glu"][sl(j, 128), sl(tt, TT)], o, o[:], ("glu", j, tt))
            if "tails" in io:
                store(io["tails"][sl(j, 128), tt, :], o, o[:, 480:512], ("tails", j, tt))

    for i in range(24):
        w, _ = load_w(wv, [(C_GT + 256 * i, 256)])
        for tt in range(NT):
            for j in range(2):
                bk = proj(w, j * 128, 128, uT, KC, tt)
                copy_epilogue(bk, 128, io["gates"][sl(2 * i + j, 128), sl(tt, TT)], ("gates", 2 * i + j, tt),
                              func=AF.Sigmoid)


A_OUT = {
    "qm": ([16, 192, NTOK], BF16), "ckvn": ([512, NTOK], BF16), "krope": ([64, NTOK], BF16),
    "qd": ([16, 128, NTOK], BF16), "kd": ([4, 128, NTOK], BF16), "vd": ([NTOK, 512], BF16),
    "qi": ([8, 128, NTOK], BF16), "ki": ([64, NTOK], BF16), "wi": ([NTOK, 16], F32),
    "glu": ([D, NTOK], F32), "gates": ([3 * D, NTOK], BF16),
}
A_IN = {
    "xT": ([D, NTOK], F32), "w_in": ([D, IN_COLS], F32), "attn_norm": ([128, KC], F32),
    "q_norm": ([128, 4], F32), "kv_norm": ([128, 4], F32), "w_uq": ([512, 3072], F32),
    "tabM_C": ([64, NTOK], F32), "tabM_S": ([64, NTOK], F32),
    "tabA_C": ([128, NTOK], F32), "tabA_S": ([128, NTOK], F32),
    "tabI_C": ([128, NTOK], F32), "tabI_S": ([128, NTOK], F32),
}


def build_phase(phase_fn, ins, outs_):
    nc = bass.Bass("TRN2", target_bir_lowering=False)
    io = {}
    for k, (shp, dt) in ins.items():
        io[k] = nc.dram_tensor(k, shp, dt, kind="ExternalInput").ap()
    for k, (shp, dt) in outs_.items():
        io[k] = nc.dram_tensor(k, shp, dt, kind="ExternalOutput").ap()
    cx = Ctx(nc)
    phase_fn(cx, io)
    cx.P.emit()
    return nc


def core_tokens(c):
    half = c % 2
    idx = np.concatenate([np.arange(g * TT, (g + 1) * TT) for g in BLOCKS[half]])
    return c // 2, idx


def vec_layout(v):
    n = v.shape[0] // 128
    return np.ascontiguousarray(v.reshape(n, 128).T)


def rope_tables(pos):
    def tab(rot):
        inv = (np.float32(THETA) ** (-np.arange(0, rot, 2, dtype=np.float32) / np.float32(rot))).astype(np.float32)
        ang = pos.astype(np.float32)[:, None] * inv[None, :]
        return np.cos(ang).astype(np.float32).T, np.sin(ang).astype(np.float32).T
    n = pos.shape[0]
    cm, sm_ = tab(64)
    ca, sa = tab(32)
    ci, si = tab(16)
    one = np.ones
    zero = np.zeros
    tM_C = np.concatenate([cm, cm], 0)
    tM_S = np.concatenate([sm_, sm_], 0)
    tA_C = np.concatenate([ca, ca, one((96, n), np.float32)], 0)
    tA_S = np.concatenate([sa, sa, zero((96, n), np.float32)], 0)
    hC = np.concatenate([ci, ci, one((48, n), np.float32)], 0)
    hS = np.concatenate([si, si, zero((48, n), np.float32)], 0)
    tI_C = np.concatenate([hC, hC], 0)
    tI_S = np.concatenate([hS, hS], 0)
    return dict(tabM_C=tM_C, tabM_S=tM_S, tabA_C=tA_C, tabA_S=tA_S, tabI_C=tI_C, tabI_S=tI_S)


KMAX = (2, 4, 6, 8)
CAST_ROT_C = ("pool", "act", "dve", "act")
NIT = 20
TOPK = 256


class SubRing:
    def __init__(self, ts):
        self.ts = ts
        self.i = 0

    def next(self):
        t = self.ts[self.i % len(self.ts)]
        t.slot = self.i % len(self.ts)
        self.i += 1
        return t


def attn_core(cx, nkb, lhs_fn, pt_ring, st_ring, acc_ring, mask_fn, vfn, ones, scale, fin):
    P = cx.P
    OT = acc_ring.next()
    LT = acc_ring.next()
    LOOK = 2
    sts = {}
    for jb in range(min(LOOK, nkb)):
        sts[jb] = st_ring.next()
        lhs_fn(jb, sts[jb])
    for jb in range(nkb):
        ST = sts.pop(jb)
        if jb + LOOK < nkb:
            sts[jb + LOOK] = st_ring.next()
            lhs_fn(jb + LOOK, sts[jb + LOOK])
        PT = pt_ring.next()
        P.op("act", lambda e, ST=ST, PT=PT: e.activation(out=PT[:], in_=ST[:], func=AF.Exp, scale=scale),
             reads=[ST.b], writes=[PT.b])
        m = mask_fn(jb)
        if m is not None:
            mt, map_ = m
            P.op("pool", lambda e, PT=PT, map_=map_: e.tensor_tensor(out=PT[:], in0=PT[:], in1=map_, op=ALU.mult),
                 reads=[PT.b, mt.b], writes=[PT.b])
        vt, vap = vfn(jb)
        P.op("pe", lambda e, PT=PT, vap=vap, jb=jb: e.matmul(OT[:], lhsT=vap, rhs=PT[:], start=(jb == 0),
                                                             stop=(jb == nkb - 1)),
             reads=[vt.b, PT.b], writes=[OT.b])
        P.op("pe", lambda e, PT=PT, jb=jb: e.matmul(LT[:], lhsT=ones[:], rhs=PT[:], start=(jb == 0),
                                                    stop=(jb == nkb - 1)),
             reads=[ones.b, PT.b], writes=[LT.b])
    fin(OT, LT)


def phase_b_mla(cx, io, ks):
    nc, P = cx.nc, cx.P
    cx.arena.reset()
    st_ring = SubRing(cx.ps.ts[0:3])
    acc_ring = SubRing(cx.ps.ts[3:7])
    misc = SubRing(cx.ps.ts[7:8])
    scale = float(192.0 ** -0.5)
    ckv = T(cx, "ckv", [128, 4, SEQ], BF16)
    kr = T(cx, "kr", [64, SEQ], BF16)
    ones = T(cx, "onesB", [128, 128], BF16)
    cm = T(cx, "cm", [128, 32, 512], BF16)
    wst = Ring(cx, "wstB", 2, [128, 4, 256], F32)
    wkb = Ring(cx, "wkb", 2, [128, 4, 256], BF16)
    KT = Ring(cx, "KT", 2, [128, SEQ], BF16)
    V = Ring(cx, "V", 2, [128, 32, 128], BF16)
    qn = Ring(cx, "qn", 2, [128, NTOK], BF16)
    qr = Ring(cx, "qr", 2, [64, NTOK], BF16)
    pt = Ring(cx, "ptB", 4, [128, 512], BF16)
    rl = Ring(cx, "rlB", 2, [128, 512], F32)
    ob = Ring(cx, "obB", 3, [128, 512], BF16)

    P.op("pool", lambda e: e.memset(ones[:], 1.0), writes=[ones.b])
    for gb in range(8):
        P.dma("sp", "bk0", ckv[:, :, sl(gb, TT)], ks["ckvn"](gb).rearrange("(kc p) n -> p kc n", p=128),
              reads=[P.dbuf("x_ckvn")], writes=[ckv.b])
        P.dma("sp", "bk1", kr[:, sl(gb, TT)], ks["krope"](gb), reads=[P.dbuf("x_krope")], writes=[kr.b])
    P.dma("sp", "bk2", cm[:], io["cmask"].rearrange("i j p q -> p (i j) q"), writes=[cm.b])
    wv = io["w_ukv"].rearrange("(kc p) c -> p kc c", p=128)
    for h in range(16):
        ws = wst.next()
        P.dma("sp", "bw%d" % ws.slot, ws[:], wv[:, :, sl(h, 256)], reads=wd(io, "w_ukv"), writes=[ws.b])
        w = wkb.next()
        P.op("pool", lambda e, w=w, ws=ws: e.tensor_copy(out=w[:], in_=ws[:]), reads=[ws.b], writes=[w.b])
        qnt = qn.next()
        qrt = qr.next()
        P.dma("sp", "bq%d" % qnt.slot, qnt[:], io["qm"][h, 0:128, :], reads=[P.dbuf("qm")], writes=[qnt.b])
        P.dma("sp", "bq%d" % qnt.slot, qrt[:], io["qm"][h, 128:192, :], reads=[P.dbuf("qm")], writes=[qrt.b])
        kt = KT.next()
        vt = V.next()
        for s8 in range(8):
            bank = misc.next()
            for kc in range(4):
                P.op("pe", lambda e, kc=kc, bank=bank, s8=s8, w=w: e.matmul(
                    bank[:], lhsT=w[:, kc, 0:128], rhs=ckv[:, kc, sl(s8, TT)], start=(kc == 0), stop=(kc == 3)),
                    reads=[w.b, ckv.b], writes=[bank.b])
            P.op("dve", lambda e, bank=bank, s8=s8, kt=kt: e.tensor_copy(out=kt[:, sl(s8, TT)], in_=bank[:]),
                 reads=[bank.b], writes=[kt.b])
        for g4 in range(8):
            bank = misc.next()
            for j in range(4):
                for kc in range(4):
                    P.op("pe", lambda e, kc=kc, bank=bank, j=j, g4=g4, w=w: e.matmul(
                        bank[:, sl(j, 128)], lhsT=ckv[:, kc, sl(g4 * 4 + j, 128)], rhs=w[:, kc, 128:256],
                        start=(kc == 0), stop=(kc == 3)), reads=[w.b, ckv.b], writes=[bank.b])
            P.op("dve", lambda e, bank=bank, g4=g4, vt=vt: e.tensor_copy(
                out=vt[:, g4 * 4:(g4 + 1) * 4, :], in_=bank[:].rearrange("p (a b) -> p a b", a=4)),
                reads=[bank.b], writes=[vt.b])
        for i in range(4):
            nkb = 4 * KMAX[i]

            def lhs_fn(jb, ST, kt=kt, qnt=qnt, qrt=qrt, i=i):
                P.op("pe", lambda e: e.matmul(ST[:], lhsT=kt[:, sl(jb, 128)], rhs=qnt[:, sl(i, TT)],
                                              start=True, stop=False), reads=[kt.b, qnt.b], writes=[ST.b])
                P.op("pe", lambda e: e.matmul(ST[:], lhsT=kr[:, sl(jb, 128)], rhs=qrt[:, sl(i, TT)],
                                              start=False, stop=True), reads=[kr.b, qrt.b], writes=[ST.b])

            def mask_fn(jb, i=i, nkb=nkb):
                if jb >= nkb - 8:
                    return cm, cm[:, i * 8 + jb - (nkb - 8), :]
                return None

            def vfn(jb, vt=vt):
                return vt, vt[:, jb, :]

            def fin(OT, LT, h=h, i=i):
                r = rl.next()
                P.op("dve", lambda e: e.reciprocal(out=r[:], in_=LT[:]), reads=[LT.b], writes=[r.b])
                o = ob.next()
                P.op("dve", lambda e: e.tensor_tensor(out=o[:], in0=OT[:], in1=r[:], op=ALU.mult),
                     reads=[OT.b, r.b], writes=[o.b])
                P.dma(STQ, "bo%d" % o.slot, io["oT_mla"][sl(h, 128), sl(i, TT)], o[:], reads=[o.b],
                      writes=[P.dbuf("oT_mla", h, i)])

            attn_core(cx, nkb, lhs_fn, pt, st_ring, acc_ring, mask_fn, vfn, ones, scale, fin)


def phase_b_dsa(cx, io, ks):
    nc, P = cx.nc, cx.P
    cx.arena.reset()
    st_ring = SubRing(cx.ps.ts[0:3])
    acc_ring = SubRing(cx.ps.ts[3:7])
    misc = SubRing(cx.ps.ts[7:8])
    scale = float(128.0 ** -0.5)
    ki2 = T(cx, "ki2", [128, SEQ], BF16)
    ones = T(cx, "onesD", [128, 128], BF16)
    ident = T(cx, "ident", [128, 128], BF16)
    score = [T(cx, "score%d" % j, [128, SEQ], F32) for j in range(4)]
    for s_ in score:
        s_.bs = [Buf("%s_%d" % (s_.name, k)) for k in range(8)]
    junk = cx.arena.alloc([128, SEQ], BF16)
    Mt = Ring(cx, "Mt", 2, [128, SEQ], BF16)
    MT = T(cx, "MT", [128, 32, 512], BF16)
    KTg = Ring(cx, "KTg", 2, [128, SEQ], BF16)
    Vg = Ring(cx, "Vg", 2, [128, 32, 128], BF16)
    pt = Ring(cx, "ptD", 4, [128, 512], BF16)
    rel = Ring(cx, "relD", 4, [128, 512], F32)
    qt = Ring(cx, "qtD", 2, [128, 512], BF16)
    qit = [T(cx, "qit%d" % j, [128, 8, 128], BF16) for j in range(4)]
    wit = [T(cx, "wit%d" % j, [128, 16], F32) for j in range(4)]
    am = Ring(cx, "am", 4, [128, 512], BF16)
    sm = [T(cx, "smD%d" % j, [128, 8], F32) for j in range(4)]
    rl = Ring(cx, "rlD", 2, [128, 512], F32)
    ob = Ring(cx, "obD", 3, [128, 512], BF16)

    P.op("pool", lambda e: e.memset(ones[:], 1.0), writes=[ones.b])
    P.dma("sp", "dk0", ident[:], io["ident"][:, :], writes=[ident.b])
    for gb in range(8):
        for hf in range(2):
            P.dma("sp", "dk1", ki2[sl(hf, 64), sl(gb, TT)], ks["ki"](gb), reads=[P.dbuf("x_ki")], writes=[ki2.b])

    for i in range(4):
        nkt = KMAX[i]
        nkb = 4 * nkt
        S_c = nkt * TT
        for j in range(4):
            tok = slice(i * TT + j * 128, i * TT + (j + 1) * 128)
            P.dma("sp", "dq%d" % j, qit[j][:], io["qi"][:, :, tok].rearrange("a p n -> p a n"),
                  reads=[P.dbuf("qi")], writes=[qit[j].b])
            P.dma("sp", "dq%d" % j, wit[j][:], io["wi"][tok, :], reads=[P.dbuf("wi")], writes=[wit[j].b])
            for hh in range(16):
                pr, hf = hh // 2, hh % 2
                for kt_ in range(nkt):
                    dots = st_ring.next()
                    P.op("pe", lambda e, dots=dots, j=j, pr=pr, hf=hf, kt_=kt_: e.matmul(
                        dots[:], lhsT=qit[j][sl(hf, 64), pr, :], rhs=ki2[sl(hf, 64), sl(kt_, TT)],
                        start=True, stop=True), reads=[qit[j].b, ki2.b], writes=[dots.b])
                    r = rel.next()
                    P.op("act", lambda e, dots=dots, r=r: e.activation(out=r[:], in_=dots[:], func=AF.Relu),
                         reads=[dots.b], writes=[r.b])
                    sc = score[j]
                    if hh == 0:
                        P.op("dve", lambda e, r=r, sc=sc, kt_=kt_, j=j: e.tensor_scalar(
                            out=sc[:, sl(kt_, TT)], in0=r[:], scalar1=wit[j][:, 0:1], scalar2=None, op0=ALU.mult),
                            reads=[r.b, wit[j].b], writes=[sc.bs[kt_]])
                    else:
                        P.op("dve", lambda e, r=r, sc=sc, kt_=kt_, j=j, hh=hh: e.scalar_tensor_tensor(
                            out=sc[:, sl(kt_, TT)], in0=r[:], scalar=wit[j][:, hh:hh + 1], in1=sc[:, sl(kt_, TT)],
                            op0=ALU.mult, op1=ALU.add), reads=[r.b, wit[j].b, sc.bs[kt_]], writes=[sc.bs[kt_]])
        for j in range(4):
            sc = score[j]
            s_ = sm[j]
            allb = sc.bs[0:nkt]
            P.op("dve", lambda e, sc=sc, s_=s_, S_c=S_c: e.tensor_reduce(out=s_[:, 0:1], in_=sc[:, 0:S_c], axis=AX.X, op=ALU.max),
                 reads=allb, writes=[s_.b])
            P.op("dve", lambda e, sc=sc, s_=s_, S_c=S_c: e.tensor_reduce(out=s_[:, 3:4], in_=sc[:, 0:S_c], axis=AX.X, op=ALU.min),
                 reads=allb, writes=[s_.b])
            P.op("dve", lambda e, s_=s_: e.tensor_tensor(out=s_[:, 2:3], in0=s_[:, 0:1], in1=s_[:, 3:4], op=ALU.subtract),
                 reads=[s_.b], writes=[s_.b])
            for k2 in range(2):
                a = am.next()
                P.dma("sp", "da%d" % a.slot, a[:], io["amask"][i, j, k2, :, :], writes=[a.b])
                kt_ = nkt - 2 + k2
                P.op("dve", lambda e, sc=sc, a=a, kt_=kt_: e.tensor_tensor(
                    out=sc[:, sl(kt_, TT)], in0=sc[:, sl(kt_, TT)], in1=a[:], op=ALU.add),
                    reads=[sc.bs[kt_], a.b], writes=[sc.bs[kt_]])
        for it in range(NIT):
            wk = float(2.0 ** -(it + 1))
            for j in range(4):
                sc, s_ = score[j], sm[j]
                allb = sc.bs[0:nkt]
                P.op("dve", lambda e, s_=s_, wk=wk: e.scalar_tensor_tensor(
                    out=s_[:, 4:5], in0=s_[:, 2:3], scalar=wk, in1=s_[:, 3:4], op0=ALU.mult, op1=ALU.add),
                    reads=[s_.b], writes=[s_.b])
            for j in range(4):
                sc, s_ = score[j], sm[j]
                allb = sc.bs[0:nkt]
                P.op("dve", lambda e, sc=sc, s_=s_, S_c=S_c: e.tensor_scalar(
                    out=junk[:, 0:S_c], in0=sc[:, 0:S_c], scalar1=s_[:, 4:5], scalar2=None,
                    op0=ALU.is_ge, op1=ALU.add, accum_out=s_[:, 5:6]), reads=allb + [s_.b], writes=[s_.b])
            for j in range(4):
                s_ = sm[j]
                P.op("dve", lambda e, s_=s_, wk=wk: e.tensor_scalar(
                    out=s_[:, 6:7], in0=s_[:, 5:6], scalar1=TOPK - 0.5, scalar2=wk, op0=ALU.is_ge, op1=ALU.mult),
                    reads=[s_.b], writes=[s_.b])
            for j in range(4):
                s_ = sm[j]
                P.op("dve", lambda e, s_=s_: e.scalar_tensor_tensor(
                    out=s_[:, 3:4], in0=s_[:, 2:3], scalar=s_[:, 6:7], in1=s_[:, 3:4], op0=ALU.mult, op1=ALU.add),
                    reads=[s_.b], writes=[s_.b])
        for j in range(4):
            sc, s_ = score[j], sm[j]
            m = Mt.next()
            P.op("dve", lambda e, sc=sc, s_=s_, m=m, S_c=S_c: e.tensor_scalar(
                out=m[:, 0:S_c], in0=sc[:, 0:S_c], scalar1=s_[:, 3:4], scalar2=None, op0=ALU.is_ge),
                reads=sc.bs[0:nkt] + [s_.b], writes=[m.b])
            if "dbg_sm" in io:
                P.dma("sp", "dbg", io["dbg_sm"][i, j, :, :], s_[:], reads=[s_.b])
                P.dma("sp", "dbg", io["dbg_M"][i, j, :, 0:S_c], m[:, 0:S_c], reads=[m.b])
                P.dma("sp", "dbg", io["dbg_sc"][i, j, :, 0:S_c], sc[:, 0:S_c], reads=sc.bs[0:nkt])
            for g4 in range(nkb // 4):
                bank = misc.next()
                bb = bank[:].bitcast(BF16)
                for b4 in range(4):
                    P.op("pe", lambda e, bb=bb, m=m, g4=g4, b4=b4: e.transpose(
                        bb[:, sl(b4, 128)], m[:, sl(g4 * 4 + b4, 128)], ident[:]),
                        reads=[m.b, ident.b], writes=[bank.b])
                P.op("act", lambda e, bb=bb, g4=g4, j=j: e.activation(
                    out=MT[:, g4 * 4:(g4 + 1) * 4, sl(j, 128)],
                    in_=bb[:, 0:512].rearrange("p (a b) -> p a b", a=4), func=AF.Copy),
                    reads=[bank.b], writes=[MT.b])
        for g in range(4):
            ktg = KTg.next()
            vg = Vg.next()
            for gb in range(nkt):
                P.dma("sp", "dK%d" % ktg.slot, ktg[:, sl(gb, TT)], ks["kd"](g, gb), reads=[P.dbuf("x_kd")],
                      writes=[ktg.b])
                P.dma("sp", "dK%d" % ktg.slot, vg[:, gb * 4:(gb + 1) * 4, :],
                      ks["vd"](g, gb).rearrange("(jb p) c -> p jb c", p=128), reads=[P.dbuf("x_vd")], writes=[vg.b])
            for hq in range(4):
                h = 4 * g + hq
                q = qt.next()
                P.dma("sp", "dQ%d" % q.slot, q[:], io["qd"][h, :, sl(i, TT)], reads=[P.dbuf("qd")], writes=[q.b])

                def lhs_fn(jb, ST, ktg=ktg, q=q):
                    P.op("pe", lambda e: e.matmul(ST[:], lhsT=ktg[:, sl(jb, 128)], rhs=q[:], start=True, stop=True),
                         reads=[ktg.b, q.b], writes=[ST.b])

                def mask_fn(jb):
                    return MT, MT[:, jb, :]

                def vfn(jb, vg=vg):
                    return vg, vg[:, jb, :]

                def fin(OT, LT, h=h, i=i):
                    r = rl.next()
                    P.op("dve", lambda e: e.reciprocal(out=r[:], in_=LT[:]), reads=[LT.b], writes=[r.b])
                    o = ob.next()
                    P.op("dve", lambda e: e.tensor_tensor(out=o[:], in0=OT[:], in1=r[:], op=ALU.mult),
                         reads=[OT.b, r.b], writes=[o.b])
                    P.dma(STQ, "do%d" % o.slot, io["oT_dsa"][sl(h, 128), sl(i, TT)], o[:], reads=[o.b],
                          writes=[P.dbuf("oT_dsa", h, i)])

                attn_core(cx, nkb, lhs_fn, pt, st_ring, acc_ring, mask_fn, vfn, ones, scale, fin)


def phase_b(cx, io):
    ks = {
        "ckvn": lambda gb: io["ckvn_g"][:, sl(gb, TT)],
        "krope": lambda gb: io["krope_g"][:, sl(gb, TT)],
        "ki": lambda gb: io["ki_g"][:, sl(gb, TT)],
        "kd": lambda g, gb: io["kd_g"][g, :, sl(gb, TT)],
        "vd": lambda g, gb: io["vd_g"][sl(gb, TT), sl(g, 128)],
    }
    phase_b_mla(cx, io, ks)
    cx.P.barrier()
    phase_b_dsa(cx, io, ks)


B_IN = {
    "qm": ([16, 192, NTOK], BF16), "qd": ([16, 128, NTOK], BF16), "qi": ([8, 128, NTOK], BF16),
    "wi": ([NTOK, 16], F32), "ckvn_g": ([512, SEQ], BF16), "krope_g": ([64, SEQ], BF16),
    "kd_g": ([4, 128, SEQ], BF16), "vd_g": ([SEQ, 512], BF16), "ki_g": ([64, SEQ], BF16),
    "w_ukv": ([512, 4096], F32), "cmask": ([4, 8, 128, 512], BF16), "amask": ([4, 4, 2, 128, 512], BF16),
    "ident": ([128, 128], BF16),
}
B_OUT = {"oT_mla": ([D, NTOK], BF16), "oT_dsa": ([D, NTOK], BF16)}


def attn_masks(half):
    cm = np.zeros((4, 8, 128, 512), np.float32)
    amk = np.zeros((4, 4, 2, 128, 512), np.float32)
    for i in range(4):
        gb = BLOCKS[half][i]
        qpos = gb * TT + np.arange(TT)
        for b in range(8):
            kpos = (KMAX[i] - 2) * TT + b * 128 + np.arange(128)
            cm[i, b] = (kpos[:, None] <= qpos[None, :]).astype(np.float32)
        for j in range(4):
            qp = gb * TT + j * 128 + np.arange(128)
            for k2 in range(2):
                kp = (KMAX[i] - 2 + k2) * TT + np.arange(TT)
                amk[i, j, k2] = np.where(kp[None, :] <= qp[:, None], 0.0, -1e30)
    return cm.astype(NPBF), amk.astype(NPBF)


def phase_c(cx, io, last):
    nc, P, ps = cx.nc, cx.P, cx.ps
    cx.arena.reset()
    xs = [T(cx, "xs%d" % k, [128, 512], F32) for k in range(KC)]
    Q = [T(cx, "Q%d" % k, [128, 16, 512], BF16) for k in range(4)]
    cvb, hc, oml, ods = Q
    mu = T(cx, "mu", [128, 16, 512], BF16)
    wst = Ring(cx, "wstC", 2, [128, KC, 256], F32)
    wb = Ring(cx, "wbC", 2, [128, KC, 256], BF16)
    yb = Ring(cx, "ybC", 3, [128, 544], F32)
    acc = Ring(cx, "accC", 4, [128, 512], F32)
    sq = Ring(cx, "sqC", 2, [128, 512], BF16)
    cw = T(cx, "cw", [128, KC, 31], F32)
    vecs = {k: T(cx, "v_" + k, [128, KC], F32) for k in ("conv_b", "ln_g", "ln_b", "mlp_norm", "final_norm")}
    ones = T(cx, "onesC", [128, 128], BF16)
    mean = T(cx, "meanC", [128, 512], F32)
    rstd = T(cx, "rstdC", [128, 512], F32)
    sm = T(cx, "smC", [128, 512], F32)
    tmp = Ring(cx, "tmpC", 4, [128, 512], F32)
    gt = Ring(cx, "gtC", 6, [128, 512], BF16)

    P.op("pool", lambda e: e.memset(ones[:], 1.0), writes=[ones.b])
    P.dma("sp", "cc0", cw[:], io["conv_w"][:, :, :], writes=[cw.b])
    for k, t in vecs.items():
        P.dma("sp", "cc1", t[:], io[k][:, :], writes=[t.b])

    def load_w(wkey, rows, cols, shape3):
        src_view3 = wviews[wkey]
        ws = wst.next()
        a, b = shape3
        wsv = ws[:].rearrange("p a b -> p (a b)").rearrange("p (a b) -> p a b", a=a)
        P.dma("sp", "cw%d" % ws.slot, wsv, src_view3[:, rows, cols], reads=wd(io, wkey), writes=[ws.b])
        w = wb.next()
        wv_ = w[:].rearrange("p a b -> p (a b)").rearrange("p (a b) -> p a b", a=a)
        ce = CAST_ROT_C[wb.i % len(CAST_ROT_C)]
        if ce == "act":
            P.op("act", lambda e: e.activation(out=wv_, in_=wsv, func=AF.Copy), reads=[ws.b], writes=[w.b])
        else:
            P.op(ce, lambda e: e.tensor_copy(out=wv_, in_=wsv), reads=[ws.b], writes=[w.b])
        return w, wv_

    def proj(wt, wv_, m0, src_t, src_fn, kcn):
        bank = ps.next()
        for kc in range(kcn):
            st_, sap = src_fn(kc)
            P.op("pe", lambda e, kc=kc, sap=sap: e.matmul(bank[:], lhsT=wv_[:, kc, m0:m0 + 128], rhs=sap,
                                                          start=(kc == 0), stop=(kc == kcn - 1)),
                 reads=[wt.b, st_.b], writes=[bank.b])
        return bank

    def rstd_from(ssbank, n_feat):
        P.op("act", lambda e: e.activation(out=sm[:], in_=ssbank[:], func=AF.Sqrt, scale=1.0 / n_feat, bias=EPS),
             reads=[ssbank.b], writes=[sm.b])
        P.op("dve", lambda e: e.reciprocal(out=rstd[:], in_=sm[:]), reads=[sm.b], writes=[rstd.b])

    def sumsq_x():
        SS = ps.next()
        for oc in range(KC):
            s_ = sq.next()
            P.op("act", lambda e, s_=s_, oc=oc: e.activation(out=s_[:], in_=xs[oc][:], func=AF.Square),
                 reads=[xs[oc].b], writes=[s_.b])
            P.op("pe", lambda e, s_=s_, oc=oc: e.matmul(SS[:], lhsT=ones[:], rhs=s_[:], start=(oc == 0),
                                                        stop=(oc == KC - 1)), reads=[ones.b, s_.b], writes=[SS.b])
        rstd_from(SS, D)

    wviews = {k: io[k].rearrange("(kc p) c -> p kc c", p=128) for k in
              ("w_o_mla", "w_o_dsa", "w_o_conv", "w_out", "w_up", "w_down")}
    xv = io["xT"].rearrange("(kc p) n -> p kc n", p=128)
    inv_d = 1.0 / D

    for tt in range(NT):
        tsl = sl(tt, TT)
        for oc in range(KC):
            P.dma("sp", "cx%d" % (oc % 4), xs[oc][:], io["xT"][sl(oc, 128), tsl], reads=[P.dbuf("xT_in")],
                  writes=[xs[oc].b])
        S1 = ps.next()
        S2 = ps.next()
        for pr in range(8):
            ybs, accs = [], []
            for c2 in range(2):
                cc = 2 * pr + c2
                y = yb.next()
                P.dma("sp", "cy%d" % y.slot, y[:, 0:32], io["halo"][sl(cc, 128), tt, :], reads=[P.dbuf("halo")],
                      writes=[y.b])
                P.dma("sp", "cy%d" % y.slot, y[:, 32:544], io["glu"][sl(cc, 128), tsl], reads=[P.dbuf("glu")],
                      writes=[y.b])
                a = acc.next()
                ybs.append(y)
                accs.append(a)
                P.op("dve", lambda e, y=y, a=a, cc=cc: e.tensor_scalar(
                    out=a[:], in0=y[:, 2:514], scalar1=cw[:, cc, 0:1], scalar2=vecs["conv_b"][:, cc:cc + 1],
                    op0=ALU.mult, op1=ALU.add), reads=[y.b, cw.b, vecs["conv_b"].b], writes=[a.b])
            for k in range(1, 31):
                for c2 in range(2):
                    cc = 2 * pr + c2
                    y, a = ybs[c2], accs[c2]
                    P.op("dve", lambda e, y=y, a=a, cc=cc, k=k: e.scalar_tensor_tensor(
                        out=a[:], in0=y[:, 2 + k:514 + k], scalar=cw[:, cc, k:k + 1], in1=a[:],
                        op0=ALU.mult, op1=ALU.add), reads=[y.b, cw.b, a.b], writes=[a.b])
            for c2 in range(2):
                cc = 2 * pr + c2
                a = accs[c2]
                P.op("act", lambda e, a=a, cc=cc: e.activation(out=cvb[:, cc, :], in_=a[:], func=AF.Copy),
                     reads=[a.b], writes=[cvb.b])
                s_ = sq.next()
                P.op("act", lambda e, a=a, s_=s_: e.activation(out=s_[:], in_=a[:], func=AF.Square),
                     reads=[a.b], writes=[s_.b])
                P.op("pe", lambda e, cc=cc: e.matmul(S1[:], lhsT=ones[:], rhs=cvb[:, cc, :], start=(cc == 0),
                                                     stop=(cc == KC - 1)), reads=[ones.b, cvb.b], writes=[S1.b])
                P.op("pe", lambda e, cc=cc, s_=s_: e.matmul(S2[:], lhsT=ones[:], rhs=s_[:], start=(cc == 0),
                                                            stop=(cc == KC - 1)), reads=[ones.b, s_.b], writes=[S2.b])
        P.op("act", lambda e: e.activation(out=mean[:], in_=S1[:], func=AF.Copy, scale=inv_d),
             reads=[S1.b], writes=[mean.b])
        msq = tmp.next()
        P.op("pool", lambda e, msq=msq: e.tensor_tensor(out=msq[:], in0=mean[:], in1=mean[:], op=ALU.mult),
             reads=[mean.b], writes=[msq.b])
        var = tmp.next()
        P.op("dve", lambda e, msq=msq, var=var: e.scalar_tensor_tensor(
            out=var[:], in0=S2[:], scalar=inv_d, in1=msq[:], op0=ALU.mult, op1=ALU.subtract),
            reads=[S2.b, msq.b], writes=[var.b])
        P.op("act", lambda e, var=var: e.activation(out=sm[:], in_=var[:], func=AF.Sqrt, scale=1.0, bias=EPS),
             reads=[var.b], writes=[sm.b])
        P.op("dve", lambda e: e.reciprocal(out=rstd[:], in_=sm[:]), reads=[sm.b], writes=[rstd.b])
        for cc in range(KC):
            t1 = tmp.next()
            P.op("pool", lambda e, t1=t1, cc=cc: e.tensor_tensor(out=t1[:], in0=cvb[:, cc, :], in1=mean[:],
                                                                 op=ALU.subtract),
                 reads=[cvb.b, mean.b], writes=[t1.b])
            P.op("pool", lambda e, t1=t1: e.tensor_tensor(out=t1[:], in0=t1[:], in1=rstd[:], op=ALU.mult),
                 reads=[t1.b, rstd.b], writes=[t1.b])
            P.op("act", lambda e, t1=t1, cc=cc: e.activation(
                out=hc[:, cc, :], in_=t1[:], func=AF.Silu, scale=vecs["ln_g"][:, cc:cc + 1],
                bias=vecs["ln_b"][:, cc:cc + 1]), reads=[t1.b, vecs["ln_g"].b, vecs["ln_b"].b], writes=[hc.b])
        P.dma("sp", "co0", oml[:], io["oT_mla"].rearrange("(kc p) n -> p kc n", p=128)[:, :, tsl],
              reads=[P.dbuf("oT_mla_in")], writes=[oml.b])
        P.dma("sp", "co1", ods[:], io["oT_dsa"].rearrange("(kc p) n -> p kc n", p=128)[:, :, tsl],
              reads=[P.dbuf("oT_dsa_in")], writes=[ods.b])
        for ocp in range(8):
            banks = {}
            for br, (wk, src) in enumerate((("w_o_mla", oml), ("w_o_dsa", ods), ("w_o_conv", hc))):
                wt, wv_ = load_w(wk, slice(0, KC), sl(ocp, 256), (KC, 256))
                for o2 in range(2):
                    banks[(br, o2)] = proj(wt, wv_, o2 * 128, src, lambda kc, src=src: (src, src[:, kc, :]), KC)
            for o2 in range(2):
                oc = 2 * ocp + o2
                ts_ = []
                for br in range(3):
                    g_ = gt.next()
                    P.dma("sp", "cg%d" % g_.slot, g_[:], io["gates"][br * D + oc * 128:br * D + (oc + 1) * 128, tsl],
                          reads=[P.dbuf("gates")], writes=[g_.b])
                    t_ = tmp.next()
                    bk = banks[(br, o2)]
                    P.op("dve", lambda e, t_=t_, bk=bk, g_=g_: e.tensor_tensor(out=t_[:], in0=bk[:], in1=g_[:],
                                                                               op=ALU.mult),
                         reads=[bk.b, g_.b], writes=[t_.b])
                    ts_.append(t_)
                P.op("pool", lambda e, a=ts_[0], b=ts_[1]: e.tensor_tensor(out=a[:], in0=a[:], in1=b[:], op=ALU.add),
                     reads=[ts_[0].b, ts_[1].b], writes=[ts_[0].b])
                P.op("pool", lambda e, a=ts_[0], b=ts_[2], oc=oc: e.tensor_tensor(out=mu[:, oc, :], in0=a[:], in1=b[:],
                                                                                 op=ALU.add),
                     reads=[ts_[0].b, ts_[2].b], writes=[mu.b])
        for ocp in range(8):
            wt, wv_ = load_w("w_out", slice(0, KC), sl(ocp, 256), (KC, 256))
            for o2 in range(2):
                oc = 2 * ocp + o2
                bk = proj(wt, wv_, o2 * 128, mu, lambda kc: (mu, mu[:, kc, :]), KC)
                P.op("dve", lambda e, bk=bk, oc=oc: e.tensor_tensor(out=xs[oc][:], in0=bk[:], in1=xs[oc][:], op=ALU.add),
                     reads=[bk.b, xs[oc].b], writes=[xs[oc].b])
        sumsq_x()
        for oc in range(KC):
            P.op("dve", lambda e, oc=oc: e.scalar_tensor_tensor(
                out=mu[:, oc, :], in0=xs[oc][:], scalar=vecs["mlp_norm"][:, oc:oc + 1], in1=rstd[:],
                op0=ALU.mult, op1=ALU.mult), reads=[xs[oc].b, vecs["mlp_norm"].b, rstd.b], writes=[mu.b])
        for fcp in range(32):
            wt, wv_ = load_w("w_up", slice(0, KC), sl(fcp, 256), (KC, 256))
            for f2 in range(2):
                fc = 2 * fcp + f2
                bk = proj(wt, wv_, f2 * 128, mu, lambda kc: (mu, mu[:, kc, :]), KC)
                r = tmp.next()
                P.op("act", lambda e, bk=bk, r=r: e.activation(out=r[:], in_=bk[:], func=AF.Relu),
                     reads=[bk.b], writes=[r.b])
                qd_ = Q[fc // 16]
                P.op("pool", lambda e, r=r, qd_=qd_, fc=fc: e.tensor_tensor(out=qd_[:, fc % 16, :], in0=r[:], in1=r[:],
                                                                          op=ALU.mult),
                     reads=[r.b], writes=[qd_.b])
        for og in range(4):
            banks = [ps.next() for _ in range(4)]
            for k8 in range(8):
                wt, wv_ = load_w("w_down", slice(k8 * 8, (k8 + 1) * 8), sl(og, 512), (8, 512))
                for kk in range(8):
                    kc = k8 * 8 + kk
                    qd_ = Q[kc // 16]
                    for o4 in range(4):
                        bk = banks[o4]
                        P.op("pe", lambda e, bk=bk, kk=kk, o4=o4, kc=kc, qd_=qd_, wv_=wv_: e.matmul(
                            bk[:], lhsT=wv_[:, kk, sl(o4, 128)], rhs=qd_[:, kc % 16, :],
                            start=(kc == 0), stop=(kc == 63)), reads=[wt.b, qd_.b], writes=[bk.b])
            for o4 in range(4):
                oc = og * 4 + o4
                bk = banks[o4]
                P.op("dve", lambda e, bk=bk, oc=oc: e.tensor_tensor(out=xs[oc][:], in0=bk[:], in1=xs[oc][:], op=ALU.add),
                     reads=[bk.b, xs[oc].b], writes=[xs[oc].b])
        if last:
            sumsq_x()
            for oc in range(KC):
                o = tmp.next()
                P.op("dve", lambda e, oc=oc, o=o: e.scalar_tensor_tensor(
                    out=o[:], in0=xs[oc][:], scalar=vecs["final_norm"][:, oc:oc + 1], in1=rstd[:],
                    op0=ALU.mult, op1=ALU.mult), reads=[xs[oc].b, vecs["final_norm"].b, rstd.b], writes=[o.b])
                P.dma(STQ, "cs%d" % o.slot, io["xT_out"][sl(oc, 128), tsl], o[:], reads=[o.b],
                      writes=[P.dbuf("xT_out", oc, tt)])
        else:
            for oc in range(KC):
                P.dma(STQ, "cs%d" % (oc % 4), io["xT_out"][sl(oc, 128), tsl], xs[oc][:], reads=[xs[oc].b],
                      writes=[P.dbuf("xT_out", oc, tt)])


C_IN = {
    "xT": ([D, NTOK], F32), "glu": ([D, NTOK], F32), "halo": ([D, 4, 32], F32), "gates": ([3 * D, NTOK], BF16),
    "oT_mla": ([D, NTOK], BF16), "oT_dsa": ([D, NTOK], BF16),
    "conv_w": ([128, KC, 31], F32), "conv_b": ([128, KC], F32), "ln_g": ([128, KC], F32), "ln_b": ([128, KC], F32),
    "mlp_norm": ([128, KC], F32), "final_norm": ([128, KC], F32),
    "w_o_mla": ([D, D], F32), "w_o_dsa": ([D, D], F32), "w_o_conv": ([D, D], F32), "w_out": ([D, D], F32),
    "w_up": ([D, DFF], F32), "w_down": ([DFF, D], F32),
}
C_OUT = {"xT_out": ([D, NTOK], F32)}


KROWS = 1152
HALO_SRC = (((None), (1, 1), (0, 1), (1, 3)), ((0, 0), (1, 0), (0, 2), (1, 2)))


def owner(gb):
    for half in range(2):
        if gb in BLOCKS[half]:
            return half, BLOCKS[half].index(gb)


def allgather(P, ins, outs, reads, writes, stream):
    P.custom_dma("pool", stream, lambda e: e.collective_compute(
        "AllGather", ALU.bypass, replica_groups=[list(range(NCORE))], ins=[ins], outs=[outs]),
        reads=reads, writes=writes, inc=1)


def phase_x(cx, io):
    nc, P = cx.nc, cx.P
    cx.arena.reset()
    selb = T(cx, "selb", [128, 8], F32)
    selh = T(cx, "selh", [128, 8], F32)
    P.dma("sp", "xs0", selb[:], io["selb"][:, :], writes=[selb.b])
    P.dma("sp", "xs0", selh[:], io["selh"][:, :], writes=[selh.b])
    allgather(P, io["ex_rows"][:, :], io["ag_rows"][:, :], [], [P.dbuf("ag_rows")], "xg0")
    allgather(P, io["ex_vd"][:, :], io["ag_vd"][:, :], [], [P.dbuf("ag_vd")], "xg1")
    allgather(P, io["ex_tails"][:, :], io["ag_tails"][:, :], [], [P.dbuf("ag_tails")], "xg2")
    cand = Ring(cx, "xcand", 4, [128, 9, 512], BF16)
    accr = Ring(cx, "xacc", 2, [128, 9, 512], BF16)
    for gb in range(8):
        half, slot = owner(gb)
        for kind in range(2):
            a = accr.next()
            if kind == 0:
                nt_, av = 9, a[:]
            else:
                nt_, av = 4, a[:, 0:4, :]
            for b in range(NB):
                r = 2 * b + half
                cnd = cand.next()
                cv = cnd[:, 0:nt_, :]
                if kind == 0:
                    src = io["ag_rows"][r * KROWS:(r + 1) * KROWS, sl(slot, TT)].rearrange("(t p) n -> p t n", p=128)
                    rd = P.dbuf("ag_rows")
                else:
                    src = io["ag_vd"][r * NTOK + slot * TT:r * NTOK + (slot + 1) * TT, :].rearrange(
                        "(t p) n -> p t n", p=128)
                    rd = P.dbuf("ag_vd")
                P.dma("sp", "xc%d" % cnd.slot, cv, src, reads=[rd], writes=[cnd.b])
                if b == 0:
                    P.op("dve", lambda e, av=av, cv=cv, r=r: e.tensor_scalar(
                        out=av, in0=cv, scalar1=selb[:, r:r + 1], scalar2=None, op0=ALU.mult),
                        reads=[cnd.b, selb.b], writes=[a.b])
                else:
                    P.op("dve", lambda e, av=av, cv=cv, r=r: e.scalar_tensor_tensor(
                        out=av, in0=cv, scalar=selb[:, r:r + 1], in1=av, op0=ALU.mult, op1=ALU.add),
                        reads=[cnd.b, selb.b, a.b], writes=[a.b])
            if kind == 0:
                dst = io["ks_rows"][:, sl(gb, TT)].rearrange("(t p) n -> p t n", p=128)
            else:
                dst = io["ks_vd"][sl(gb, TT), :].rearrange("(t p) n -> p t n", p=128)
            P.dma(STQ, "xo%d" % a.slot, dst, av, reads=[a.b], writes=[P.dbuf("ks", gb, kind)])
    tl = T(cx, "xtl", [128, 8, KC, 128], F32)
    for r in range(8):
        P.dma("sp", "xt0", tl[:, r, :, :], io["ag_tails"][r * D:(r + 1) * D, :].rearrange("(c p) n -> p c n", p=128),
              reads=[P.dbuf("ag_tails")], writes=[tl.b])
    hl = T(cx, "xhl", [128, KC, 128], F32)
    P.op("dve", lambda e: e.memset(hl[:], 0.0), writes=[hl.b])
    for i in range(4):
        for p in range(2):
            src = HALO_SRC[p][i]
            if src is None:
                continue
            half_, s_ = src
            for b in range(NB):
                r = 2 * b + half_
                k = p * 4 + b
                P.op("dve", lambda e, i=i, r=r, s_=s_, k=k: e.scalar_tensor_tensor(
                    out=hl[:, :, sl(i, 32)], in0=tl[:, r, :, sl(s_, 32)], scalar=selh[:, k:k + 1],
                    in1=hl[:, :, sl(i, 32)], op0=ALU.mult, op1=ALU.add), reads=[tl.b, selh.b, hl.b], writes=[hl.b])
    P.dma(STQ, "xh0", io["halo"].rearrange("(c p) i n -> p c (i n)", p=128), hl[:], reads=[hl.b],
          writes=[P.dbuf("halo_w")])


AG_OVERLAP = False
WSHARD = True
WNAMES = {"w_in": (D, IN_COLS), "mla_w_uq": (512, 3072), "mla_w_ukv": (512, 4096), "w_o_mla": (D, D),
          "w_o_dsa": (D, D), "w_o_conv": (D, D), "w_out": (D, D), "w_up": (D, DFF), "w_down": (DFF, D)}
VNAMES = ("attn_norm", "mla_q_norm", "mla_kv_norm", "conv_b_dw", "conv_ln_g", "conv_ln_b", "mlp_norm")
DEPTH = 2


def build_fused():
    nc = bass.Bass("TRN2", target_bir_lowering=False)
    cx = Ctx(nc)
    P = cx.P

    def ext(name, shp, dt, out=False):
        return nc.dram_tensor(name, shp, dt, kind="ExternalOutput" if out else "ExternalInput").ap()

    def internal(name, shp, dt):
        return nc.dram_tensor(name, shp, dt).ap()

    g = {}
    g["xT"] = ext("xT", [D, NTOK], F32)
    g["out"] = ext("out", [D, NTOK], F32, out=True)
    for k, (shp, dt) in A_IN.items():
        if k.startswith("tab"):
            g[k] = ext(k, shp, dt)
    g["cmask"] = ext("cmask", [4, 8, 128, 512], BF16)
    g["amask"] = ext("amask", [4, 4, 2, 128, 512], BF16)
    g["ident"] = ext("ident", [128, 128], BF16)
    g["selb"] = ext("selb", [128, 8], F32)
    g["selh"] = ext("selh", [128, 8], F32)
    g["final_norm"] = ext("final_norm", [128, KC], F32)
    g["conv_w"] = ext("conv_w", [DEPTH, 128, KC, 31], F32)
    for k in VNAMES:
        n = 4 if k in ("mla_q_norm", "mla_kv_norm") else KC
        g[k] = ext(k, [DEPTH, 128, n], F32)
    W = {}
    WB = {}
    if WSHARD:
        if AG_OVERLAP:
            P.nobar.update(("wg0", "wg1"))
        sh, bounce = {}, {}
        for k, (r, c) in WNAMES.items():
            sh[k] = ext(k, [DEPTH, r // NCORE, c], F32)
            bounce[k] = internal(k + "_b", [DEPTH, r // NCORE, c], F32)
            W[k] = internal(k + "_f", [DEPTH, r, c], F32)
        for L in range(DEPTH):
            for k in WNAMES:
                P.dma("sp", "wg0", bounce[k][L], sh[k][L], writes=[P.dbuf("wb", k, L)])
                WB[(k, L)] = P.dbuf("wf", k, L)
                allgather(P, bounce[k][L], W[k][L], [P.dbuf("wb", k, L)], [WB[(k, L)]], "wg1")
    else:
        for k, (r, c) in WNAMES.items():
            W[k] = ext(k, [DEPTH, r, c], F32)
    sc = {k: internal("s_" + k, shp, dt) for k, (shp, dt) in A_OUT.items() if k in ("qm", "qd", "qi", "wi", "glu", "gates")}
    ex_rows = internal("ex_rows", [KROWS, NTOK], BF16)
    ex_vd = internal("ex_vd", [NTOK, 512], BF16)
    ex_tails = internal("ex_tails", [D, 128], F32)
    ag_rows = internal("ag_rows", [NCORE * KROWS, NTOK], BF16)
    ag_vd = internal("ag_vd", [NCORE * NTOK, 512], BF16)
    ag_tails = internal("ag_tails", [NCORE * D, 128], F32)
    ks_rows = internal("ks_rows", [KROWS, SEQ], BF16)
    ks_vd = internal("ks_vd", [SEQ, 512], BF16)
    halo = internal("halo", [D, 4, 32], F32)
    oT_mla = internal("oT_mla", [D, NTOK], BF16)
    oT_dsa = internal("oT_dsa", [D, NTOK], BF16)
    xmid = internal("xmid", [D, NTOK], F32)
    P.barrier()
    for L in range(DEPTH):
        last = L == DEPTH - 1
        x_in = g["xT"] if L == 0 else xmid
        ioA = {"xT": x_in, "w_in": W["w_in"][L], "attn_norm": g["attn_norm"][L], "q_norm": g["mla_q_norm"][L],
               "kv_norm": g["mla_kv_norm"][L], "w_uq": W["mla_w_uq"][L],
               "ckvn": ex_rows[0:512, :], "krope": ex_rows[512:576, :],
               "kd": ex_rows[576:1088, :].rearrange("(g p) n -> g p n", p=128), "ki": ex_rows[1088:1152, :],
               "vd": ex_vd, "tails": ex_tails.rearrange("c (i n) -> c i n", i=4)}
        for k in ("tabM_C", "tabM_S", "tabA_C", "tabA_S", "tabI_C", "tabI_S"):
            ioA[k] = g[k]
        ioA.update(sc)
        if WSHARD:
            ioA["_wdeps"] = {"w_in": WB[("w_in", L)], "w_uq": WB[("mla_w_uq", L)]}
        cx.arena.reset()
        phase_a(cx, ioA)
        P.barrier()
        ioX = {"selb": g["selb"], "selh": g["selh"], "ex_rows": ex_rows, "ag_rows": ag_rows, "ex_vd": ex_vd,
               "ag_vd": ag_vd, "ex_tails": ex_tails, "ag_tails": ag_tails, "ks_rows": ks_rows, "ks_vd": ks_vd,
               "halo": halo}
        phase_x(cx, ioX)
        P.barrier()
        ioB = {"qm": sc["qm"], "qd": sc["qd"], "qi": sc["qi"], "wi": sc["wi"],
               "ckvn_g": ks_rows[0:512, :], "krope_g": ks_rows[512:576, :],
               "kd_g": ks_rows[576:1088, :].rearrange("(g p) n -> g p n", p=128), "ki_g": ks_rows[1088:1152, :],
               "vd_g": ks_vd, "w_ukv": W["mla_w_ukv"][L], "cmask": g["cmask"], "amask": g["amask"],
               "ident": g["ident"], "oT_mla": oT_mla, "oT_dsa": oT_dsa}
        if WSHARD:
            ioB["_wdeps"] = {"w_ukv": WB[("mla_w_ukv", L)]}
        phase_b(cx, ioB)
        P.barrier()
        ioC = {"xT": x_in, "glu": sc["glu"], "halo": halo, "gates": sc["gates"], "oT_mla": oT_mla, "oT_dsa": oT_dsa,
               "conv_w": g["conv_w"][L], "conv_b": g["conv_b_dw"][L], "ln_g": g["conv_ln_g"][L],
               "ln_b": g["conv_ln_b"][L], "mlp_norm": g["mlp_norm"][L], "final_norm": g["final_norm"],
               "xT_out": g["out"] if last else xmid}
        for k in ("w_o_mla", "w_o_dsa", "w_o_conv", "w_out", "w_up", "w_down"):
            ioC[k] = W[k][L]
        if WSHARD:
            ioC["_wdeps"] = {k: WB[(k, L)] for k in ("w_o_mla", "w_o_dsa", "w_o_conv", "w_out", "w_up", "w_down")}
        phase_c(cx, ioC, last)
        P.barrier()
    P.emit()
    return nc


_PROGS = {}


def _prog(name):
    if name not in _PROGS:
        if name == "A":
            _PROGS[name] = build_phase(phase_a, A_IN, A_OUT)
        elif name == "B":
            _PROGS[name] = build_phase(phase_b, B_IN, B_OUT)
        elif name == "C0":
            _PROGS[name] = build_phase(lambda cx, io: phase_c(cx, io, False), C_IN, C_OUT)
        elif name == "C1":
            _PROGS[name] = build_phase(lambda cx, io: phase_c(cx, io, True), C_IN, C_OUT)
    return _PROGS[name]


def _gather_global(resA, name, b, axis):
    parts = [None] * 8
    for half in range(2):
        a = np.asarray(resA[2 * b + half][name])
        for i, gb in enumerate(BLOCKS[half]):
            parts[gb] = np.take(a, np.arange(i * TT, (i + 1) * TT), axis=axis)
    return np.ascontiguousarray(np.concatenate(parts, axis=axis))


def _halo(resA, b):
    full = _gather_global(resA, "glu", b, 1)
    out = []
    for half in range(2):
        h = np.zeros((D, 4, 32), np.float32)
        for i, gb in enumerate(BLOCKS[half]):
            if gb > 0:
                h[:, i, :] = full[:, gb * TT - 32:gb * TT]
        out.append(h)
    return out


def kernel_unfused(x, attn_norm, w_in, mla_q_norm, mla_kv_norm, mla_w_uq, mla_w_ukv,
           conv_w_dw, conv_b_dw, conv_ln_g, conv_ln_b,
           w_o_mla, w_o_dsa, w_o_conv, w_out, mlp_norm, w_up, w_down, final_norm):
    f32 = lambda a: np.ascontiguousarray(np.asarray(a, dtype=np.float32))
    x = f32(x)
    cores = list(range(NCORE))
    tok = [core_tokens(c) for c in cores]
    xT = [np.ascontiguousarray(x[b, idx, :].T) for (b, idx) in tok]
    tabs = [rope_tables(idx) for (_, idx) in tok]
    masks = [attn_masks(h) for h in range(2)]
    ident = np.eye(128, dtype=np.float32).astype(NPBF)
    depth = np.asarray(w_in).shape[0]
    for L in range(depth):
        last = L == depth - 1
        in_maps = []
        for c in cores:
            m = {"xT": xT[c], "w_in": f32(w_in[L]), "attn_norm": vec_layout(f32(attn_norm[L])),
                 "q_norm": vec_layout(f32(mla_q_norm[L])), "kv_norm": vec_layout(f32(mla_kv_norm[L])),
                 "w_uq": f32(mla_w_uq[L])}
            m.update(tabs[c])
            in_maps.append(m)
        rA = run_bass_kernel_spmd(_prog("A"), in_maps, core_ids=cores).results
        in_maps = []
        gl = {}
        for b in range(NB):
            gl[b] = {"ckvn_g": _gather_global(rA, "ckvn", b, 1), "krope_g": _gather_global(rA, "krope", b, 1),
                     "kd_g": _gather_global(rA, "kd", b, 2), "vd_g": _gather_global(rA, "vd", b, 0),
                     "ki_g": _gather_global(rA, "ki", b, 1)}
        for c in cores:
            b, half = c // 2, c % 2
            m = {"qm": np.asarray(rA[c]["qm"]), "qd": np.asarray(rA[c]["qd"]), "qi": np.asarray(rA[c]["qi"]),
                 "wi": np.asarray(rA[c]["wi"]), "w_ukv": f32(mla_w_ukv[L]),
                 "cmask": masks[half][0], "amask": masks[half][1], "ident": ident}
            m.update(gl[b])
            in_maps.append(m)
        rB = run_bass_kernel_spmd(_prog("B"), in_maps, core_ids=cores).results
        halos = {b: _halo(rA, b) for b in range(NB)}
        in_maps = []
        for c in cores:
            b, half = c // 2, c % 2
            m = {"xT": xT[c], "glu": np.asarray(rA[c]["glu"]), "halo": halos[b][half],
                 "gates": np.asarray(rA[c]["gates"]), "oT_mla": np.asarray(rB[c]["oT_mla"]),
                 "oT_dsa": np.asarray(rB[c]["oT_dsa"]),
                 "conv_w": np.ascontiguousarray(f32(conv_w_dw[L]).reshape(31, KC, 128).transpose(2, 1, 0)),
                 "conv_b": vec_layout(f32(conv_b_dw[L])), "ln_g": vec_layout(f32(conv_ln_g[L])),
                 "ln_b": vec_layout(f32(conv_ln_b[L])), "mlp_norm": vec_layout(f32(mlp_norm[L])),
                 "final_norm": vec_layout(f32(final_norm)),
                 "w_o_mla": f32(w_o_mla[L]), "w_o_dsa": f32(w_o_dsa[L]), "w_o_conv": f32(w_o_conv[L]),
                 "w_out": f32(w_out[L]), "w_up": f32(w_up[L]), "w_down": f32(w_down[L])}
            in_maps.append(m)
        rC = run_bass_kernel_spmd(_prog("C1" if last else "C0"), in_maps, core_ids=cores).results
        xT = [np.ascontiguousarray(np.asarray(rC[c]["xT_out"], dtype=np.float32)) for c in cores]
    out = np.empty((NB, SEQ, D), np.float32)
    for c in cores:
        b, idx = tok[c]
        out[b, idx, :] = xT[c].T
    return out


_FUSED = {}


def kernel(x, attn_norm, w_in, mla_q_norm, mla_kv_norm, mla_w_uq, mla_w_ukv,
           conv_w_dw, conv_b_dw, conv_ln_g, conv_ln_b,
           w_o_mla, w_o_dsa, w_o_conv, w_out, mlp_norm, w_up, w_down, final_norm):
    f32 = lambda a: np.ascontiguousarray(np.asarray(a, dtype=np.float32))
    x = f32(x)
    cores = list(range(NCORE))
    tok = [core_tokens(c) for c in cores]
    masks = [attn_masks(h) for h in range(2)]
    ident = np.eye(128, dtype=np.float32).astype(NPBF)
    wts = {"w_in": f32(w_in), "mla_w_uq": f32(mla_w_uq), "mla_w_ukv": f32(mla_w_ukv), "w_o_mla": f32(w_o_mla),
           "w_o_dsa": f32(w_o_dsa), "w_o_conv": f32(w_o_conv), "w_out": f32(w_out), "w_up": f32(w_up),
           "w_down": f32(w_down)}
    vecs = {"attn_norm": attn_norm, "mla_q_norm": mla_q_norm, "mla_kv_norm": mla_kv_norm, "conv_b_dw": conv_b_dw,
            "conv_ln_g": conv_ln_g, "conv_ln_b": conv_ln_b, "mlp_norm": mlp_norm}
    vecs = {k: np.stack([vec_layout(f32(v)[L]) for L in range(DEPTH)]) for k, v in vecs.items()}
    conv_w = np.stack([np.ascontiguousarray(f32(conv_w_dw)[L].reshape(31, KC, 128).transpose(2, 1, 0))
                       for L in range(DEPTH)])
    fn = vec_layout(f32(final_norm))
    in_maps = []
    for c in cores:
        b, idx = tok[c]
        half = c % 2
        m = {"xT": np.ascontiguousarray(x[b, idx, :].T), "cmask": masks[half][0], "amask": masks[half][1],
             "ident": ident, "final_norm": fn, "conv_w": conv_w}
        m.update(rope_tables(idx))
        selb = np.zeros((128, 8), np.float32)
        selb[:, 2 * b] = 1.0
        selb[:, 2 * b + 1] = 1.0
        selh = np.zeros((128, 8), np.float32)
        selh[:, half * 4 + b] = 1.0
        m["selb"] = selb
        m["selh"] = selh
        m.update(vecs)
        for k, w in wts.items():
            if WSHARD:
                r = w.shape[1] // NCORE
                m[k] = np.ascontiguousarray(w[:, c * r:(c + 1) * r, :])
            else:
                m[k] = w
        in_maps.append(m)
    if "nc" not in _FUSED:
        _FUSED["nc"] = build_fused()
    res = run_bass_kernel_spmd(_FUSED["nc"], in_maps, core_ids=cores).results
    out = np.empty((NB, SEQ, D), np.float32)
    for c in cores:
        b, idx = tok[c]
        out[b, idx, :] = np.asarray(res[c]["out"], dtype=np.float32).T
    return out
```
